# Optimizing a Trainium2 kernel written in Bass

```python
import numpy as np
import jax
import jax.numpy as jnp
from jax import lax

D_MODEL = 1024
BATCH = 32
SEQ = 2048
DEPTH = 1
DEC_BATCH = 8
DEC_SEQ = 2048
PAST_LEN = 128

HEAD_DIM = 64
A_HEADS = 8
A_KV_HEADS = 2
A_GROUP = A_HEADS // A_KV_HEADS
B_HEADS = 8
A_WIDTH = A_HEADS * HEAD_DIM
A_KV_WIDTH = A_KV_HEADS * HEAD_DIM
B_WIDTH = B_HEADS * HEAD_DIM
MIX_WIDTH = A_WIDTH + B_WIDTH
WINDOW = 128
A_BLOCK = 128
ROT_DIM = HEAD_DIM // 4
ROPE_THETA = 500000.0
GRID_W = 64
NA_ROWS_MAX = 8
NA_COLS = 16
NA_COL_BLOCK = 16
NA_KEY_COLS = 32
NA_N_COL_BLOCKS = GRID_W // NA_COL_BLOCK
PLE_DIM = 256
EPS = 1e-6
IN_WIDTHS = (A_WIDTH, A_KV_WIDTH, A_KV_WIDTH, A_WIDTH, B_WIDTH, B_WIDTH, B_WIDTH, B_WIDTH)
IN_WIDTH = sum(IN_WIDTHS)
IN_SPLITS = [int(v) for v in np.cumsum(IN_WIDTHS)[:-1]]

kernel_name = 'hymba_style_window_gqa_neighbourhood_encoder'


def rms_norm(x, g):
    xf = x.astype(jnp.float32)
    y = xf * lax.rsqrt(jnp.mean(xf * xf, axis=-1, keepdims=True) + EPS)
    return (y * g.astype(jnp.float32)).astype(x.dtype)


def partial_rope(x, positions):
    inv_freq = jnp.power(ROPE_THETA, -jnp.arange(0, ROT_DIM, 2, dtype=jnp.float32) / ROT_DIM)
    ang = positions.astype(jnp.float32)[:, None] * inv_freq[None, :]
    cos = jnp.cos(ang)[:, None, :]
    sin = jnp.sin(ang)[:, None, :]
    xr = x[..., :ROT_DIM].astype(jnp.float32)
    x1, x2 = xr[..., :ROT_DIM // 2], xr[..., ROT_DIM // 2:]
    rot = jnp.concatenate([x1 * cos - x2 * sin, x2 * cos + x1 * sin], axis=-1)
    return jnp.concatenate([rot.astype(x.dtype), x[..., ROT_DIM:]], axis=-1)


def windowed_gqa_sink(q, k, v, sink):
    b, s = q.shape[0], q.shape[1]
    nb = s // A_BLOCK
    span = A_BLOCK + 2 * WINDOW
    pad = ((0, 0), (WINDOW, WINDOW), (0, 0), (0, 0))
    kp = jnp.pad(k, pad)
    vp = jnp.pad(v, pad)
    qi = np.arange(A_BLOCK)[:, None]
    si = np.arange(span)[None, :]
    band = np.abs(si - WINDOW - qi) <= WINDOW
    sink_f = sink.astype(jnp.float32).reshape(A_KV_HEADS, A_GROUP)[None, :, :, None, None]
    scale = HEAD_DIM ** -0.5

    def block(j):
        start = j * A_BLOCK
        qb = lax.dynamic_slice_in_dim(q, start, A_BLOCK, axis=1)
        qb = qb.reshape(b, A_BLOCK, A_KV_HEADS, A_GROUP, HEAD_DIM)
        kb = lax.dynamic_slice_in_dim(kp, start, span, axis=1)
        vb = lax.dynamic_slice_in_dim(vp, start, span, axis=1)
        sc = jnp.einsum('bqkgd,bskd->bkgqs', qb, kb).astype(jnp.float32) * scale
        kpos = start - WINDOW + jnp.arange(span)
        valid = band & ((kpos >= 0) & (kpos < s))[None, :]
        sc = jnp.where(valid, sc, -jnp.inf)
        m = jnp.maximum(jnp.max(sc, axis=-1, keepdims=True), sink_f)
        e = jnp.exp(sc - m)
        denom = jnp.sum(e, axis=-1, keepdims=True) + jnp.exp(sink_f - m)
        pr = (e / denom).astype(vb.dtype)
        o = jnp.einsum('bkgqs,bskd->bqkgd', pr, vb)
        return o.reshape(b, A_BLOCK, A_WIDTH)

    out = lax.map(block, jnp.arange(nb))
    return out.transpose(1, 0, 2, 3).reshape(b, s, A_WIDTH)


def neighbourhood_attn(q, k, v, rpb):
    b, s = q.shape[0], q.shape[1]
    rows = s // GRID_W
    wr = min(NA_ROWS_MAX, rows)
    qg = q.reshape(b, rows, GRID_W, B_HEADS, HEAD_DIM)
    kg = k.reshape(b, rows, GRID_W, B_HEADS, HEAD_DIM)
    vg = v.reshape(b, rows, GRID_W, B_HEADS, HEAD_DIM)
    cols = np.arange(GRID_W)
    col_start = np.clip(cols - NA_COLS // 2, 0, GRID_W - NA_COLS)
    blk_start = np.clip(np.arange(NA_N_COL_BLOCKS) * NA_COL_BLOCK - NA_COLS // 2,
                        0, GRID_W - NA_KEY_COLS)
    key_cols = blk_start[:, None] + np.arange(NA_KEY_COLS)[None, :]
    q_cols = cols.reshape(NA_N_COL_BLOCKS, NA_COL_BLOCK)
    qs = col_start.reshape(NA_N_COL_BLOCKS, NA_COL_BLOCK)[:, :, None]
    kc = key_cols[:, None, :]
    col_valid = (kc >= qs) & (kc < qs + NA_COLS)
    dcol = np.clip(kc - q_cols[:, :, None], -(NA_COLS - 1), NA_COLS - 1) + NA_COLS - 1
    rpb_cols = rpb[:, :, dcol]
    mask = col_valid[:, None, :, None, :]
    scale = HEAD_DIM ** -0.5

    def row(r):
        rs = jnp.clip(r - wr // 2, 0, rows - wr)
        qr = lax.dynamic_index_in_dim(qg, r, axis=1, keepdims=False)
        qr = qr.reshape(b, NA_N_COL_BLOCKS, NA_COL_BLOCK, B_HEADS, HEAD_DIM)
        kr = lax.dynamic_slice_in_dim(kg, rs, wr, axis=1)
        vr = lax.dynamic_slice_in_dim(vg, rs, wr, axis=1)
        kb = kr[:, :, key_cols]
        vb = vr[:, :, key_cols]
        sc = jnp.einsum('bnqhd,banchd->bnhqac', qr, kb).astype(jnp.float32) * scale
        drow = rs + jnp.arange(wr) - r + NA_ROWS_MAX - 1
        bias = rpb_cols[:, drow].transpose(2, 0, 3, 1, 4)
        sc = jnp.where(mask, sc + bias.astype(jnp.float32), -jnp.inf)
        sc = sc.reshape(b, NA_N_COL_BLOCKS, B_HEADS, NA_COL_BLOCK, wr * NA_KEY_COLS)
        pr = jax.nn.softmax(sc, axis=-1)
        pr = pr.reshape(b, NA_N_COL_BLOCKS, B_HEADS, NA_COL_BLOCK, wr, NA_KEY_COLS).astype(vb.dtype)
        o = jnp.einsum('bnhqac,banchd->bnqhd', pr, vb)
        return o.reshape(b, GRID_W, B_WIDTH)

    out = lax.map(row, jnp.arange(rows))
    return out.transpose(1, 0, 2, 3).reshape(b, s, B_WIDTH)


def encoder_layer(x, p, norm_w, w_in, q_norm_a, k_norm_a, sink_a,
                  q_norm_b, k_norm_b, rpb_b, w_out, w_ple, w_ple_gate):
    b, s, _ = x.shape
    h = rms_norm(x, norm_w)
    proj = h @ w_in
    q_a, k_a, v_a, g_a, q_b, k_b, v_b, g_b = jnp.split(proj, IN_SPLITS, axis=-1)
    pos = jnp.arange(s)
    q_a = partial_rope(rms_norm(q_a.reshape(b, s, A_HEADS, HEAD_DIM), q_norm_a), pos)
    k_a = partial_rope(rms_norm(k_a.reshape(b, s, A_KV_HEADS, HEAD_DIM), k_norm_a), pos)
    v_a = v_a.reshape(b, s, A_KV_HEADS, HEAD_DIM)
    o_a = windowed_gqa_sink(q_a, k_a, v_a, sink_a) * jax.nn.silu(g_a)
    q_b = rms_norm(q_b.reshape(b, s, B_HEADS, HEAD_DIM), q_norm_b)
    k_b = rms_norm(k_b.reshape(b, s, B_HEADS, HEAD_DIM), k_norm_b)
    v_b = v_b.reshape(b, s, B_HEADS, HEAD_DIM)
    o_b = neighbourhood_attn(q_b, k_b, v_b, rpb_b) * jax.nn.silu(g_b)
    x = x + jnp.concatenate([o_a, o_b], axis=-1) @ w_out
    gate = jax.nn.sigmoid(x @ w_ple_gate)
    return x + (p @ w_ple) * gate


def run_trunk(x, p, norm_w, w_in, q_norm_a, k_norm_a, sink_a,
              q_norm_b, k_norm_b, rpb_b, w_out, w_ple, w_ple_gate):
    for i in range(DEPTH):
        x = encoder_layer(x, p[i], norm_w[i], w_in[i], q_norm_a[i], k_norm_a[i], sink_a[i],
                          q_norm_b[i], k_norm_b[i], rpb_b[i], w_out[i], w_ple[i], w_ple_gate[i])
    return x


def setup_inputs(seed: int = 0) -> dict:
    key = jax.random.key(seed)
    ks = jax.random.split(key, 16)
    f32 = jnp.float32
    nrm = jax.random.normal
    return {
        'x_prompt': nrm(ks[0], (BATCH, SEQ, D_MODEL), f32),
        'x_sample': nrm(ks[1], (DEC_BATCH, DEC_SEQ, D_MODEL), f32),
        'p_prompt': nrm(ks[2], (DEPTH, BATCH, SEQ, PLE_DIM), f32),
        'p_sample': nrm(ks[3], (DEPTH, DEC_BATCH, DEC_SEQ, PLE_DIM), f32),
        'norm_w': 1.0 + 0.02 * nrm(ks[4], (DEPTH, D_MODEL), f32),
        'w_in': nrm(ks[5], (DEPTH, D_MODEL, IN_WIDTH), f32) * D_MODEL ** -0.5,
        'q_norm_a': 1.0 + 0.02 * nrm(ks[6], (DEPTH, HEAD_DIM), f32),
        'k_norm_a': 1.0 + 0.02 * nrm(ks[7], (DEPTH, HEAD_DIM), f32),
        'sink_a': 0.5 * nrm(ks[8], (DEPTH, A_HEADS), f32),
        'q_norm_b': 1.0 + 0.02 * nrm(ks[9], (DEPTH, HEAD_DIM), f32),
        'k_norm_b': 1.0 + 0.02 * nrm(ks[10], (DEPTH, HEAD_DIM), f32),
        'rpb_b': 0.1 * nrm(ks[11], (DEPTH, B_HEADS, 2 * NA_ROWS_MAX - 1, 2 * NA_COLS - 1), f32),
        'w_out': nrm(ks[12], (DEPTH, MIX_WIDTH, D_MODEL), f32) * MIX_WIDTH ** -0.5,
        'w_ple': nrm(ks[13], (DEPTH, PLE_DIM, D_MODEL), f32) * PLE_DIM ** -0.5,
        'w_ple_gate': nrm(ks[14], (DEPTH, D_MODEL, D_MODEL), f32) * D_MODEL ** -0.5,
    }


def reference(x_prompt, x_sample, p_prompt, p_sample, norm_w, w_in, q_norm_a, k_norm_a, sink_a,
              q_norm_b, k_norm_b, rpb_b, w_out, w_ple, w_ple_gate):
    y_prompt = run_trunk(x_prompt, p_prompt, norm_w, w_in, q_norm_a, k_norm_a, sink_a,
                         q_norm_b, k_norm_b, rpb_b, w_out, w_ple, w_ple_gate)
    y_sample = run_trunk(x_sample, p_sample, norm_w, w_in, q_norm_a, k_norm_a, sink_a,
                         q_norm_b, k_norm_b, rpb_b, w_out, w_ple, w_ple_gate)
    return (y_prompt, y_sample)
```

```python
import numpy as np
from contextlib import ExitStack
import concourse.bass as bass
import concourse.mybir as mybir
from concourse.bass_utils import run_bass_kernel_spmd

F32 = mybir.dt.float32
BF16 = mybir.dt.bfloat16
AF = mybir.ActivationFunctionType
ALU = mybir.AluOpType
AX = mybir.AxisListType

D = 1024
S = 2048
NT = 16
INW = 3328
EPS = 1e-6
KR = 8
QR = 5
TRANSITIVE = True

WMAP = [(0, 0, 640), (640, 1280, 1024), (1664, 640, 128), (1792, 2304, 512),
        (2304, 768, 512), (2816, 2816, 512)]


def _b_tiles(t):
    out = []
    for u in range(NT):
        pat = []
        for kp in range(2):
            for qp in range(2):
                r = 2 * t + qp
                rs = min(max(r - 4, 0), 24)
                kr = 2 * u + kp
                pat.append(rs <= kr <= rs + 7)
        if any(pat):
            out.append((u, (u - t, tuple(pat))))
    return out


def _variants():
    vs = []
    for t in range(NT):
        for _, v in _b_tiles(t):
            if v not in vs:
                vs.append(v)
    return vs


VARIANTS = _variants()
NV = len(VARIANTS)


class Sched:
    def __init__(self, nc, es):
        self.nc = nc
        self.es = es
        self.eng = {'pe': nc.tensor, 'act': nc.scalar, 'dve': nc.vector, 'pool': nc.gpsimd, 'sp': nc.sync}
        self.sem = {}
        self.cnt = {}
        self.isdma = {}
        self.known = {e: {} for e in self.eng}
        self.res = {}
        self.pending = None
        self.snap = {}
        self.nstand = 0
        self.transitive = TRANSITIVE
        for e in self.eng:
            if e != 'sp':
                self.add_sem(e, False)

    def add_sem(self, name, isdma):
        self.sem[name] = self.es.enter_context(self.nc.semaphore("s_" + name))
        self.cnt[name] = 0
        self.isdma[name] = isdma

    def _need(self, e, ev):
        if ev is None:
            return
        src, v = ev
        if e == 'pe' and src == 'pe':
            return
        if self.isdma[src]:
            v = self.cnt[src]
        if self.known[e].get(src, 0) >= v:
            return
        if self.pending is not None:
            self.pending.append((src, v))
        else:
            self.eng[e].wait_ge(self.sem[src], v)
            self.nstand += 1
        self.known[e][src] = v
        if self.transitive:
            sn = self.snap.get((src, v))
            if sn:
                ke = self.known[e]
                for k2, v2 in sn.items():
                    if ke.get(k2, 0) < v2:
                        ke[k2] = v2

    def _deps(self, e, reads, writes):
        for r in reads:
            st = self.res.get(r)
            if st:
                self._need(e, st[0])
        for w in writes:
            st = self.res.get(w)
            if st:
                self._need(e, st[0])
                for src, v in list(st[1].items()):
                    self._need(e, (src, v))

    def _record(self, ev, reads, writes):
        src, v = ev
        for r in reads:
            st = self.res.setdefault(r, [None, {}])
            if st[1].get(src, 0) < v:
                st[1][src] = v
        for w in writes:
            self.res[w] = [ev, {}]

    def _collect(self, e, reads, writes):
        self.pending = []
        self._deps(e, reads, writes)
        waits, self.pending = self.pending, None
        last = {}
        for src, v in waits:
            last[src] = max(last.get(src, 0), v)
        waits = list(last.items())
        if self.transitive and len(waits) > 1:
            keep = []
            for i, (src, v) in enumerate(waits):
                implied = False
                for j, (s2, v2) in enumerate(waits):
                    if i == j:
                        continue
                    sn = self.snap.get((s2, v2))
                    if sn and sn.get(src, 0) >= v:
                        sn_i = self.snap.get((src, v))
                        if sn_i and sn_i.get(s2, 0) >= v2 and i < j:
                            continue
                        implied = True
                        break
                if not implied:
                    keep.append((src, v))
            waits = keep
        for src, v in waits[:-1]:
            self.eng[e].wait_ge(self.sem[src], v)
            self.nstand += 1
        return waits[-1] if waits else None

    def op(self, e, reads, writes, fn):
        w = self._collect(e, reads, writes)
        inst = fn(self.eng[e])
        if w is not None:
            inst._wait_ge(self.sem[w[0]], w[1])
        self.cnt[e] += 1
        inst.then_inc(self.sem[e], 1)
        self.snap[(e, self.cnt[e])] = dict(self.known[e])
        self._record((e, self.cnt[e]), reads, writes)

    def pe_group(self, reads, writes, fns):
        w = self._collect('pe', reads, writes)
        inst = None
        for i, fn in enumerate(fns):
            inst = fn(self.eng['pe'])
            if i == 0 and w is not None:
                inst._wait_ge(self.sem[w[0]], w[1])
        self.cnt['pe'] += 1
        inst.then_inc(self.sem['pe'], 1)
        self.snap[('pe', self.cnt['pe'])] = dict(self.known['pe'])
        self._record(('pe', self.cnt['pe']), reads, writes)

    def dma(self, q, chan, reads, writes, out, in_, serialize=True, **kw):
        if chan not in self.sem:
            self.add_sem(chan, True)
        if serialize and self.cnt[chan] > 0:
            self._need(q, (chan, self.cnt[chan]))
        self._deps(q, reads, writes)
        inst = self.eng[q].dma_start(out=out, in_=in_, **kw)
        self.cnt[chan] += 16
        inst.then_inc(self.sem[chan], 16)
        self._record((chan, self.cnt[chan]), reads, writes)

    def barrier(self, engines=None):
        for e in (engines or self.eng):
            for src in self.sem:
                if self.cnt[src] > 0:
                    self._need(e, (src, self.cnt[src]))


def build(nseq_p=4, nseq_s=1):
    nc = bass.Bass("TRN2", target_bir_lowering=False)
    nseq = nseq_p + nseq_s

    def din(name, shape):
        return nc.dram_tensor(name, list(shape), F32, kind="ExternalInput")

    xp = din("xp", [max(nseq_p, 1), S, D]).ap()
    pp = din("pp", [max(nseq_p, 1), S, 256]).ap()
    yp = nc.dram_tensor("yp", [max(nseq_p, 1), S, D], F32, kind="ExternalOutput").ap()
    if nseq_s:
        xs = din("xs", [nseq_s, S, D]).ap()
        pps = din("ps", [nseq_s, S, 256]).ap()
        ys = nc.dram_tensor("ys", [nseq_s, S, D], F32, kind="ExternalOutput").ap()
    norm_w = din("norm_w", [D]).ap()
    w_in = din("w_in", [D, INW]).ap()
    qna = din("q_norm_a", [64])
    kna = din("k_norm_a", [64])
    sink = din("sink_a", [8])
    qnb = din("q_norm_b", [64])
    knb = din("k_norm_b", [64])
    rpb = din("rpb_b", [120, 31]).ap()
    w_out = din("w_out", [D, D]).ap()
    w_ple = din("w_ple", [256, D]).ap()
    w_gate = din("w_gate", [D, D]).ap()
    c_ident = din("c_ident", [128, 128]).ap()
    c_j2 = din("c_j2", [128, 128]).ap()
    c_cmask = din("c_cmask", [128, 128]).ap()
    c_maska = din("c_maska", [128, 2, 128]).ap()
    c_ropec = din("c_ropec", [128, NT * 16]).ap()
    c_ropes = din("c_ropes", [128, NT * 16]).ap()
    pd = nc.dram_tensor("pd_scratch", [120, 128], F32, kind="Internal")

    def seq_aps(s):
        if s < nseq_p:
            return xp[s], pp[s], yp[s]
        return xs[s - nseq_p], pps[s - nseq_p], ys[s - nseq_p]

    with ExitStack() as es:
        sc = Sched(nc, es)

        def sb(name, shape, dt, stack=es):
            return stack.enter_context(nc.sbuf_tensor(name, list(shape), dt))

        Wg = sb("Wg", [128, 8, INW], BF16)
        Wo = sb("Wo", [128, 8, D], BF16)
        Wgt = sb("Wgt", [128, 8, D], BF16)
        Wp = sb("Wp", [128, 2, D], BF16)
        EB = sb("EB", [128, NV, 8, 128], BF16)
        idb = sb("idb", [128, 128], BF16)
        maskA = sb("maskA", [128, 2, 128], BF16)
        ropeC = sb("ropeC", [128, NT, 16], F32)
        ropeS = sb("ropeS", [128, NT, 16], F32)
        gpp = sb("gpp", [128, 4], F32)
        gain16 = sb("gain16", [128, 10, 16], F32)
        esink = sb("esink", [128, 8], F32)
        normw = sb("normw", [128, 8], F32)
        Qz = sb("Qz", [128, QR, 16, 128], BF16)
        Kt = sb("Kt", [128, KR, 5, 128], BF16)
        Vr = sb("Vr", [128, KR, 10, 65], BF16)
        Gr = sb("Gr", [128, QR, D], BF16)
        ps_mm = es.enter_context(nc.psum_tensor("ps_mm", [128, 2, 512], F32))
        ps_s = es.enter_context(nc.psum_tensor("ps_s", [128, 2, 1024], F32))
        ps_o = es.enter_context(nc.psum_tensor("ps_o", [128, 2, 512], F32))

        mmc = [0]

        def mm_bank():
            b = mmc[0] % 2
            mmc[0] += 1
            return b

        with ExitStack() as ies:
            wst2 = sb("wst2", [128, 2, INW], F32, ies)
            z = sb("z", [120, 128], F32, ies)
            Gt4 = sb("Gt4", [128, 3, 8, 128], F32, ies)
            ebf2 = sb("ebf2", [128, 1, 8, 128], F32, ies)
            j2 = sb("j2", [128, 128], F32, ies)
            cmask = sb("cmask", [128, 128], F32, ies)
            idf = sb("idf", [128, 128], F32, ies)
            mAf = sb("mAf", [128, 2, 128], F32, ies)
            g16 = sb("g16", [128, 2, 16], F32, ies)
            nw8 = sb("nw8", [8, 128], F32, ies)

            sc.op('pool', [], ['z'], lambda e: e.memset(z[:], 0.0))
            sc.op('pool', [], ['gpp'], lambda e: e.memset(gpp[:], 1.0))

            def mdma(chan, writes, out, in_, **kw):
                sc.dma('sp', chan, [], writes, out, in_, serialize=False, **kw)

            mdma('misc', ['z'], z[:, 48:79], rpb)
            mdma('misc', ['nw8'], nw8[:], norm_w.rearrange("(k p) -> k p", p=128))
            mdma('misc', ['idf'], idf[:], c_ident)
            mdma('misc', ['j2'], j2[:], c_j2)
            mdma('misc', ['cmask'], cmask[:], c_cmask)
            sc.dma('sp', 'pdw', ['z'], ['pd'], pd.ap(), z[:])
            sc.op('pool', [], ['Qz'], lambda e: e.memset(Qz[:], 0.0))
            sc.op('pool', [], ['Vr'], lambda e: e.memset(Vr[:], 1.0))
            sc.pe_group(['nw8', 'idf'], [('psmm', 0)], [
                lambda e: e.matmul(ps_mm[:, 0, 0:8], lhsT=nw8[0:8, :], rhs=idf[0:8, 0:8], start=True, stop=True)])
            sc.op('dve', [('psmm', 0)], ['normw'], lambda e: e.tensor_copy(out=normw[:], in_=ps_mm[:, 0, 0:8]))
            sc.op('dve', ['idf'], ['idb'], lambda e: e.tensor_copy(out=idb[:], in_=idf[:]))

            gn = sb("gn", [4, 128], F32, ies)

            def emit_misc2():
                mdma('misc2', ['mAf'], mAf[:], c_maska)
                mdma('misc2', ['ropeC'], ropeC[:].rearrange("p t c -> p (t c)"), c_ropec)
                mdma('misc2', ['ropeS'], ropeS[:].rearrange("p t c -> p (t c)"), c_ropes)
                mdma('misc2', ['esink'], esink[:], bass.AP(sink, 0, [[0, 128], [1, 8]]))
                mdma('misc2', ['g16'], g16[:, 0, :], bass.AP(qna, 0, [[0, 128], [1, 16]]))
                mdma('misc2', ['g16'], g16[:, 1, :], bass.AP(kna, 0, [[0, 128], [1, 16]]))
                for row, src in enumerate([qna, kna, qnb, knb]):
                    for half in range(2):
                        mdma('misc2', ['gn'], gn[row:row + 1, half * 64:(half + 1) * 64], bass.AP(src, 0, [[0, 1], [1, 64]]))

            def gen_w():
                wi = 0
                for kc in range(8):
                    i = wi % 2
                    wi += 1
                    wst = wst2[:, i, :]
                    sc.dma('sp', 'wst%d' % i, [], [('wst', i)], wst, w_in[kc * 128:(kc + 1) * 128, :])
                    for (dst, src, w) in WMAP:
                        if dst < 1664:
                            sc.op('dve', [('wst', i), 'normw'], ['W'], lambda e, dst=dst, src=src, w=w, kc=kc, wst=wst: e.tensor_scalar(
                                out=Wg[:, kc, dst:dst + w], in0=wst[:, src:src + w], scalar1=normw[:, kc:kc + 1],
                                scalar2=None, op0=ALU.mult))
                        else:
                            sc.op('act', [('wst', i), 'normw'], ['W'], lambda e, dst=dst, src=src, w=w, kc=kc, wst=wst: e.activation(
                                out=Wg[:, kc, dst:dst + w], in_=wst[:, src:src + w], func=AF.Copy, scale=normw[:, kc:kc + 1]))
                    if kc == 1:
                        emit_misc2()
                    yield

            def eb_dma(vi):
                delta, pat = VARIANTS[vi]
                gb = vi % 3
                for qp in range(2):
                    for kp in range(2):
                        drow = min(max(2 * delta + 7 + kp - qp, 0), 14)
                        sc.dma('act', 'gt%d' % gb, ['pd'], [('Gt', gb)],
                               Gt4[qp * 64:(qp + 1) * 64, gb, :, kp * 64:(kp + 1) * 64],
                               bass.AP(pd, drow * 128, [[1, 64], [15 * 128, 8], [1, 64]]), serialize=False)

            def gen_eb():
                for vi, (delta, pat) in enumerate(VARIANTS):
                    bb = vi % 2
                    gb = vi % 3
                    Gt = Gt4[:, gb]
                    ebf = ebf2[:, 0]
                    for half in range(2):
                        sc.pe_group([('Gt', gb), 'j2'], [('pss', bb)], [
                            (lambda e, h=h: e.matmul(ps_s[:, bb, h * 128:(h + 1) * 128], lhsT=Gt[:, h, :], rhs=j2[:],
                                                     start=True, stop=True)) for h in range(half * 4, half * 4 + 4)])
                    sc.op('act', [('pss', bb)], [('ebf', 0)], lambda e: e.activation(
                        out=ebf.rearrange("p h q -> p (h q)"), in_=ps_s[:, bb, :], func=AF.Exp))
                    sc.op('dve', [('ebf', 0), 'cmask'], ['EB'], lambda e, vi=vi: e.tensor_tensor(
                        out=EB[:, vi, :, :], in0=ebf, in1=cmask[:].unsqueeze(1).to_broadcast([128, 8, 128]), op=ALU.mult))
                    for kp in range(2):
                        for qp in range(2):
                            if not pat[kp * 2 + qp]:
                                sc.op('pool', [], ['EB'], lambda e, vi=vi, kp=kp, qp=qp: e.memset(
                                    EB[kp * 64:(kp + 1) * 64, vi, :, qp * 64:(qp + 1) * 64], 0.0))
                    if vi + 3 < NV:
                        eb_dma(vi + 3)
                    yield

            for vi in range(min(3, NV)):
                eb_dma(vi)
            gens = [gen_w(), gen_eb()]
            while gens:
                for gen in list(gens):
                    try:
                        next(gen)
                    except StopIteration:
                        gens.remove(gen)
            sc.pe_group(['gn', 'idf'], [('psmm', 1)], [
                lambda e: e.matmul(ps_mm[:, 1, 0:4], lhsT=gn[0:4, :], rhs=idf[0:4, 0:4], start=True, stop=True)])
            sc.op('dve', [('psmm', 1)], ['gpp'], lambda e: e.tensor_copy(out=gpp[:], in_=ps_mm[:, 1, 0:4]))
            for half in range(2):
                sc.op('dve', ['gpp'], ['gpp'], lambda e, half=half: e.memset(gpp[half * 64:half * 64 + 16, 0:2], 1.0))
            sc.op('act', ['esink'], ['esink'], lambda e: e.activation(out=esink[:], in_=esink[:], func=AF.Exp))
            sc.op('dve', ['mAf'], ['maskA'], lambda e: e.tensor_copy(out=maskA[:], in_=mAf[:]))
            sc.op('dve', ['g16'], ['gain16'], lambda e: e.tensor_copy(
                out=gain16[:, 0:8, :], in_=g16[:, 0:1, :].to_broadcast([128, 8, 16])))
            sc.op('dve', ['g16'], ['gain16'], lambda e: e.tensor_copy(
                out=gain16[:, 8:10, :], in_=g16[:, 1:2, :].to_broadcast([128, 2, 16])))
            sc.barrier()
        xt = sb("xt", [128, 1, D], F32)
        xT = sb("xT", [128, 8, 128], BF16)
        uA = sb("uA", [128, 640], F32)
        uB = sb("uB", [128, 1024], F32)
        sq = sb("sq", [128, 1024], F32)
        qn2A = sb("qn2A", [128, 640], BF16)
        qn2B = sb("qn2B", [128, 1024], BF16)
        xb = qn2B
        otmp = sb("otmp", [128, 512], F32)
        r0 = sb("r0", [128, 10, 16], F32)
        r1 = sb("r1", [128, 10, 16], F32)
        r2 = sb("r2", [128, 10, 16], F32)
        stx = sb("stx", [128, 8], F32)
        sth = sb("sth", [128, 3, 16], F32)
        den = sb("den", [128, 2, 8], F32)
        PT = sb("PT", [128, 3, 1024], BF16)
        x1b = sb("x1b", [128, D], BF16)
        mix = sb("mix", [128, D], BF16)
        mT = sb("mT", [128, 8, 128], BF16)
        xr = sb("xr", [128, 2, D], F32)
        pt = sb("pt", [128, 2, 256], F32)
        pb = sb("pb", [128, 256], BF16)
        pT = sb("pT", [128, 2, 128], BF16)

        def psT(bank):
            return ps_mm[:, bank, :].bitcast(BF16)

        def transposes(src_blocks, bank, reads):
            pv = psT(bank)
            sc.pe_group(reads + ['idb'], [('psmm', bank)], [
                (lambda e, i=i, blk=blk: e.transpose(out=pv[:, i * 128:(i + 1) * 128], in_=blk, identity=idb[:]))
                for i, blk in enumerate(src_blocks)])
            return pv

        NG = nseq * NT

        def g_aps(g):
            xa, pa, ya = seq_aps(g // NT)
            t = g % NT
            return xa[t * 128:(t + 1) * 128, :], pa[t * 128:(t + 1) * 128, :], ya[t * 128:(t + 1) * 128, :], t

        def load_x(g):
            xa, _, _, _ = g_aps(g)
            sc.dma('sp', 'xt0', [], [('xt', 0)], xt[:, 0, :], xa)

        def load_r(g):
            xa, _, _, _ = g_aps(g)
            slot = g % 2
            sc.dma('sp', 'xr%d' % slot, [], [('xr', slot)], xr[:, slot, :], xa)

        def proj(g):
            _, _, _, j = g_aps(g)
            qs, ks = g % QR, g % KR
            xs_ = xt[:, 0, :]
            sc.op('act', [('xt', 0)], ['sq', 'sqhi', 'stx'], lambda e: e.activation(
                out=sq[:], in_=xs_, func=AF.Square, accum_out=stx[:, 0:1]))
            sc.op('dve', [('xt', 0)], ['qn2B'], lambda e: e.tensor_copy(out=xb[:], in_=xs_))
            if g + 1 < NG:
                load_x(g + 1)
            sc.op('dve', ['stx'], ['stx1'], lambda e: e.tensor_scalar(
                out=stx[:, 1:2], in0=stx[:, 0:1], scalar1=1.0 / D, scalar2=EPS, op0=ALU.mult, op1=ALU.add))
            sc.op('act', ['stx1'], ['stx2'], lambda e: e.activation(out=stx[:, 2:3], in_=stx[:, 1:2], func=AF.Ln))
            sc.op('act', ['stx2'], ['stx3'], lambda e: e.activation(out=stx[:, 3:4], in_=stx[:, 2:3], func=AF.Exp, scale=-0.5))
            sc.op('dve', ['stx1'], ['stx4'], lambda e: e.tensor_scalar(
                out=stx[:, 4:5], in0=stx[:, 1:2], scalar1=EPS, scalar2=None, op0=ALU.mult))
            sc.op('dve', ['stx3'], ['stx5'], lambda e: e.tensor_scalar(
                out=stx[:, 5:6], in0=stx[:, 3:4], scalar1=-1.0, scalar2=None, op0=ALU.mult))
            rstd = stx[:, 3:4]
            nrstd = stx[:, 5:6]
            bank = mm_bank()
            pv = transposes([xb[:, i * 128:(i + 1) * 128] for i in range(8)], bank, ['qn2B'])
            sc.op('dve', [('psmm', bank)], ['xT'], lambda e: e.tensor_copy(
                out=xT[:].rearrange("p k t -> p (k t)"), in_=pv[:, 0:1024]))
            yield

            def group(c0, w):
                bank = mm_bank()
                sc.pe_group(['xT', 'W'], [('psmm', bank)], [
                    (lambda e, kc=kc: e.matmul(ps_mm[:, bank, 0:w], lhsT=xT[:, kc, :], rhs=Wg[:, kc, c0:c0 + w],
                                               start=(kc == 0), stop=(kc == 7))) for kc in range(8)])
                return bank

            def qknorm(ubuf, uname, nh, eps_ap):
                w = nh * 64
                sc.op('act', [uname], ['sq', 'sqhi'], lambda e: e.activation(out=sq[:, 0:w], in_=ubuf[:, 0:w], func=AF.Square))
                sc.op('dve', ['sq', 'sqhi'], ['sth0'], lambda e: e.reduce_sum(
                    out=sth[:, 0, 0:nh], in_=sq[:, 0:w].rearrange("p (h d) -> p h d", d=64), axis=AX.X))
                sc.op('dve', ['sth0', 'stx4'], ['sth0'], lambda e: e.tensor_scalar(
                    out=sth[:, 0, 0:nh], in0=sth[:, 0, 0:nh], scalar1=1.0 / 64, scalar2=eps_ap, op0=ALU.mult, op1=ALU.add))
                sc.op('act', ['sth0'], ['sth1'], lambda e: e.activation(out=sth[:, 1, 0:nh], in_=sth[:, 0, 0:nh], func=AF.Ln))
                sc.op('act', ['sth1'], ['sth2'], lambda e: e.activation(
                    out=sth[:, 2, 0:nh], in_=sth[:, 1, 0:nh], func=AF.Exp, scale=-0.5))

            b0 = group(0, 512)
            sc.op('act', [('psmm', b0)], ['uA'], lambda e: e.activation(out=uA[:, 0:512], in_=ps_mm[:, b0, :], func=AF.Copy))
            yield
            b1 = group(512, 128)
            sc.op('act', [('psmm', b1)], ['uA'], lambda e: e.activation(out=uA[:, 512:640], in_=ps_mm[:, b1, 0:128], func=AF.Copy))
            qknorm(uA, 'uA', 10, stx[:, 4:5])
            u3 = uA[:, 0:640].rearrange("p (h d) -> p h d", d=64)
            sc.op('dve', ['uA', 'sth2'], ['qn2A'], lambda e: e.tensor_tensor(
                out=qn2A[:, 0:512].rearrange("p (g kv d) -> p kv g d", kv=2, d=64),
                in0=uA[:, 0:512].rearrange("p (kv g d) -> p kv g d", kv=2, d=64),
                in1=sth[:, 2, 0:8].rearrange("p (kv g) -> p kv g", kv=2).unsqueeze(3).to_broadcast([128, 2, 4, 64]),
                op=ALU.mult))
            sc.op('dve', ['uA', 'sth2'], ['qn2A'], lambda e: e.tensor_tensor(
                out=qn2A[:, 512:640].rearrange("p (h d) -> p h d", d=64), in0=u3[:, 8:10, :],
                in1=sth[:, 2, 8:10].unsqueeze(2).to_broadcast([128, 2, 64]), op=ALU.mult))
            sc.op('dve', ['uA', 'sth2'], ['r0'], lambda e: e.tensor_tensor(
                out=r0[:], in0=u3[:, :, 0:16], in1=sth[:, 2, 0:10].unsqueeze(2).to_broadcast([128, 10, 16]), op=ALU.mult))
            sc.op('pool', ['r0', 'gain16'], ['r0'], lambda e: e.tensor_tensor(out=r0[:], in0=r0[:], in1=gain16[:], op=ALU.mult))
            sc.op('pool', ['r0', 'ropeC'], ['r1'], lambda e: e.tensor_tensor(
                out=r1[:], in0=r0[:], in1=ropeC[:, j:j + 1, :].to_broadcast([128, 10, 16]), op=ALU.mult))
            sc.op('pool', ['r0', 'ropeS'], ['r2'], lambda e: e.tensor_tensor(
                out=r2[:, :, 0:8], in0=r0[:, :, 8:16], in1=ropeS[:, j:j + 1, 0:8].to_broadcast([128, 10, 8]), op=ALU.mult))
            sc.op('pool', ['r0', 'ropeS'], ['r2'], lambda e: e.tensor_tensor(
                out=r2[:, :, 8:16], in0=r0[:, :, 0:8], in1=ropeS[:, j:j + 1, 8:16].to_broadcast([128, 10, 8]), op=ALU.mult))
            sc.op('pool', ['r1', 'r2'], ['qn2A'], lambda e: e.tensor_tensor(
                out=qn2A[:, 0:512].rearrange("p (g kv d) -> p kv g d", kv=2, d=64)[:, :, :, 0:16],
                in0=r1[:, 0:8, :].rearrange("p (kv g) d -> p kv g d", kv=2),
                in1=r2[:, 0:8, :].rearrange("p (kv g) d -> p kv g d", kv=2), op=ALU.add))
            sc.op('pool', ['r1', 'r2'], ['qn2A'], lambda e: e.tensor_tensor(
                out=qn2A[:, 512:640].rearrange("p (h d) -> p h d", d=64)[:, :, 0:16],
                in0=r1[:, 8:10, :], in1=r2[:, 8:10, :], op=ALU.add))
            yield
            for gi in range(2):
                bq = group(640 + gi * 512, 512)
                sc.op('act', [('psmm', bq)], ['uB'], lambda e, bq=bq, gi=gi: e.activation(
                    out=uB[:, gi * 512:(gi + 1) * 512], in_=ps_mm[:, bq, :], func=AF.Copy))
                if gi == 0:
                    yield
            qknorm(uB, 'uB', 16, stx[:, 4:5])
            sc.op('dve', ['uB', 'sth2'], ['qn2B'], lambda e: e.tensor_tensor(
                out=qn2B[:].rearrange("p (h d) -> p h d", d=64), in0=uB[:].rearrange("p (h d) -> p h d", d=64),
                in1=sth[:, 2, 0:16].unsqueeze(2).to_broadcast([128, 16, 64]), op=ALU.mult))
            yield
            bv = group(1664, 512)
            sc.op('act', [('psmm', bv), 'stx3'], [('V', ks)], lambda e: e.activation(
                out=Vr[:, ks, 0:8, 0:64], in_=ps_mm[:, bv, :].rearrange("p (h d) -> p h d", d=64),
                func=AF.Identity, scale=rstd))
            yield
            bv2 = group(2176, 128)
            sc.op('act', [('psmm', bv2), 'stx3'], [('V', ks)], lambda e: e.activation(
                out=Vr[:, ks, 8:10, 0:64], in_=ps_mm[:, bv2, 0:128].rearrange("p (h d) -> p h d", d=64),
                func=AF.Identity, scale=rstd))
            for gi in range(2):
                gtmp = sq[:, gi * 512:(gi + 1) * 512]
                zcb = uB[:, 0:512] if gi == 0 else uA[:, 0:512]
                zname = 'uB' if gi == 0 else 'uA'
                sqn = 'sq' if gi == 0 else 'sqhi'
                bg = group(2304 + gi * 512, 512)
                sc.op('act', [('psmm', bg), 'stx5'], [sqn], lambda e, bg=bg, gtmp=gtmp: e.activation(
                    out=gtmp, in_=ps_mm[:, bg, :], func=AF.Exp, scale=nrstd))
                sc.op('dve', [('psmm', bg), 'stx3', sqn], [zname], lambda e, bg=bg, zcb=zcb: e.tensor_scalar(
                    out=zcb, in0=ps_mm[:, bg, :], scalar1=rstd, scalar2=None, op0=ALU.mult))
                sc.op('act', [sqn], [sqn], lambda e, gtmp=gtmp: e.activation(out=gtmp, in_=gtmp, func=AF.Ln, bias=1.0))
                sc.op('act', [sqn], [sqn], lambda e, gtmp=gtmp: e.activation(out=gtmp, in_=gtmp, func=AF.Exp, scale=-1.0))
                sc.op('pool', [zname, sqn], [('G', qs)], lambda e, gi=gi, gtmp=gtmp, zcb=zcb: e.tensor_tensor(
                    out=Gr[:, qs, gi * 512:(gi + 1) * 512], in0=zcb, in1=gtmp, op=ALU.mult))
                yield
            bank = mm_bank()
            pv = transposes([qn2A[:, i * 128:(i + 1) * 128] for i in range(5)], bank, ['qn2A'])
            pv3 = pv[:, 0:640].rearrange("p (b t) -> p b t", t=128)
            sc.op('dve', [('psmm', bank), 'gpp'], [('Qz', qs)], lambda e: e.tensor_scalar(
                out=Qz[0:64, qs, 0:4, :], in0=pv3[0:64, 0:4, :], scalar1=gpp[0:64, 0:1], scalar2=None, op0=ALU.mult))
            sc.op('dve', [('psmm', bank), 'gpp'], [('Qz', qs)], lambda e: e.tensor_scalar(
                out=Qz[64:128, qs, 4:8, :], in0=pv3[64:128, 0:4, :], scalar1=gpp[64:128, 0:1], scalar2=None, op0=ALU.mult))
            sc.op('dve', [('psmm', bank), 'gpp'], [('K', ks)], lambda e: e.tensor_scalar(
                out=Kt[:, ks, 0, :], in0=pv3[:, 4, :], scalar1=gpp[:, 1:2], scalar2=None, op0=ALU.mult))
            yield
            bank = mm_bank()
            pv = transposes([qn2B[:, i * 128:(i + 1) * 128] for i in range(8)], bank, ['qn2B'])
            pv3 = pv[:, 0:1024].rearrange("p (b t) -> p b t", t=128)
            qz_b = Qz[:, qs, 8:16, :].rearrange("p (i two) t -> p i two t", two=2)
            sc.op('act', [('psmm', bank), 'gpp'], [('Qz', qs)], lambda e: e.activation(
                out=qz_b[0:64, :, 0, :], in_=pv3[0:64, 0:4, :], func=AF.Copy, scale=gpp[0:64, 2:3]))
            sc.op('act', [('psmm', bank), 'gpp'], [('Qz', qs)], lambda e: e.activation(
                out=qz_b[64:128, :, 1, :], in_=pv3[64:128, 0:4, :], func=AF.Copy, scale=gpp[64:128, 2:3]))
            sc.op('act', [('psmm', bank), 'gpp'], [('K', ks)], lambda e: e.activation(
                out=Kt[:, ks, 1:5, :], in_=pv3[:, 4:8, :], func=AF.Copy, scale=gpp[:, 3:4]))
            yield

        sreg = [0]
        rot = [0]

        def o_finish(with_sink, qs, col0):
            o4 = ps_o[:, :, 0:260].rearrange("p b (h c) -> p b h c", c=65)
            d3 = den[:, 0, :].rearrange("p (b h) -> p b h", b=2)
            if with_sink:
                sc.op('dve', ['pso', 'esink'], ['den0'], lambda e: e.tensor_tensor(
                    out=d3, in0=o4[:, :, :, 64], in1=esink[:].rearrange("p (b h) -> p b h", b=2), op=ALU.add))
            else:
                sc.op('dve', ['pso'], ['den0'], lambda e: e.tensor_copy(out=d3, in_=o4[:, :, :, 64]))
            sc.op('dve', ['den0'], ['den1'], lambda e: e.reciprocal(out=den[:, 1, :], in_=den[:, 0, :]))
            r3 = den[:, 1, :].rearrange("p (b h) -> p b h", b=2)
            sc.op('dve', ['pso', 'den1'], ['otmp'], lambda e: e.tensor_tensor(
                out=otmp[:].rearrange("p (b h d) -> p b h d", b=2, d=64), in0=o4[:, :, :, 0:64],
                in1=r3.unsqueeze(3).to_broadcast([128, 2, 4, 64]), op=ALU.mult))
            sc.op('dve', ['otmp', ('G', qs)], ['mix'], lambda e: e.tensor_tensor(
                out=mix[:, col0:col0 + 512], in0=otmp[:], in1=Gr[:, qs, col0:col0 + 512], op=ALU.mult))

        def units(g):
            xa_t, pa_t, ya_t, t = g_aps(g)
            base = (g // NT) * NT
            qs = g % QR
            rslot = g % 2
            xrs = xr[:, rslot, :]
            sc.dma('sp', 'pt%d' % rslot, [], [('pt', rslot)], pt[:, rslot, :], pa_t)
            units = []
            for b in (t - 1, t, t + 1):
                if 0 <= b < NT:
                    units.append(('A', b, None if b == t else (0 if b < t else 1)))
            nA = len(units)
            for (uu, var) in _b_tiles(t):
                units.append(('B', uu, VARIANTS.index(var)))
            nU = len(units)
            regs = {}
            pts = {}

            def emit_S(i):
                kind, kt, _ = units[i]
                reg = sreg[0] % 2
                sreg[0] += 1
                regs[i] = reg
                ks = (base + kt) % KR
                if kind == 'A':
                    sc.pe_group([('K', ks), ('Qz', qs)], [('pss', reg)], [
                        (lambda e, kv=kv: e.matmul(ps_s[:, reg, kv * 512:(kv + 1) * 512], lhsT=Kt[:, ks, 0, :],
                                                   rhs=Qz[:, qs, 4 * kv:4 * kv + 4, :].rearrange("p h t -> p (h t)"),
                                                   start=True, stop=True)) for kv in range(2)])
                else:
                    for half in range(2):
                        sc.pe_group([('K', ks), ('Qz', qs)], [('pss', reg)], [
                            (lambda e, pi=pi: e.matmul(ps_s[:, reg, pi * 256:(pi + 1) * 256], lhsT=Kt[:, ks, 1 + pi, :],
                                                       rhs=Qz[:, qs, 8 + 2 * pi:8 + 2 * pi + 2, :].rearrange("p h t -> p (h t)"),
                                                       start=True, stop=True))
                            for pi in range(half * 2, half * 2 + 2)])

            def emit_E(i):
                kind, kt, aux = units[i]
                reg = regs[i]
                r = rot[0] % 3
                rot[0] += 1
                pts[i] = r
                if kind == 'A':
                    sc.op('act', [('pss', reg)], [('PT', r)], lambda e: e.activation(
                        out=PT[:, r, :], in_=ps_s[:, reg, :], func=AF.Exp, scale=0.125))
                    if aux is not None:
                        sc.op('dve', [('PT', r), 'maskA'], [('PT', r)], lambda e: e.tensor_tensor(
                            out=PT[:, r, :].rearrange("p (h q) -> p h q", q=128),
                            in0=PT[:, r, :].rearrange("p (h q) -> p h q", q=128),
                            in1=maskA[:, aux:aux + 1, :].to_broadcast([128, 8, 128]), op=ALU.mult))
                else:
                    sc.op('act', [('pss', reg)], [('PT', r)], lambda e: e.activation(
                        out=PT[:, r, :], in_=ps_s[:, reg, :], func=AF.Exp, scale=0.125))
                    sc.op('dve', [('PT', r), 'EB'], [('PT', r)], lambda e: e.tensor_tensor(
                        out=PT[:, r, :], in0=PT[:, r, :], in1=EB[:, aux, :, :].rearrange("p h q -> p (h q)"), op=ALU.mult))

            def emit_PV(i):
                kind, kt, _ = units[i]
                r = pts[i]
                ks = (base + kt) % KR
                if kind == 'A':
                    first, last = (i == 0), (i == nA - 1)
                    sc.pe_group([('PT', r), ('V', ks)], ['pso'], [
                        (lambda e, h=h: e.matmul(ps_o[:, h // 4, (h % 4) * 65:(h % 4) * 65 + 65],
                                                 lhsT=PT[:, r, h * 128:(h + 1) * 128], rhs=Vr[:, ks, h // 4, :],
                                                 start=(first and h % 4 == 0), stop=last,
                                                 skip_group_check=True)) for h in range(8)])
                else:
                    first, last = (i == nA), (i == nU - 1)
                    sc.pe_group([('PT', r), ('V', ks)], ['pso'], [
                        (lambda e, h=h: e.matmul(ps_o[:, h // 4, (h % 4) * 65:(h % 4) * 65 + 65],
                                                 lhsT=PT[:, r, h * 128:(h + 1) * 128], rhs=Vr[:, ks, 2 + h, :],
                                                 start=(first and h % 4 == 0), stop=last,
                                                 skip_group_check=True)) for h in range(8)])

            emit_S(0)
            yield
            for k in range(1, nU + 3):
                if k < nU:
                    emit_S(k)
                if k - 1 < nU:
                    emit_E(k - 1)
                if 0 <= k - 3:
                    emit_PV(k - 3)
                    if k - 3 == nA - 1:
                        o_finish(True, qs, 0)
                yield
            o_finish(False, qs, 512)
            yield

        def tail(g):
            xa_t, pa_t, ya_t, t = g_aps(g)
            base = (g // NT) * NT
            qs = g % QR
            rslot = g % 2
            xrs = xr[:, rslot, :]
            sc.op('pool', [('pt', rslot)], ['pb'], lambda e: e.tensor_copy(out=pb[:], in_=pt[:, rslot, :]))
            bank = mm_bank()
            pv = transposes([mix[:, i * 128:(i + 1) * 128] for i in range(8)], bank, ['mix'])
            sc.op('dve', [('psmm', bank)], ['mT'], lambda e: e.tensor_copy(
                out=mT[:].rearrange("p k t -> p (k t)"), in_=pv[:, 0:1024]))
            bank = mm_bank()
            pv = transposes([pb[:, i * 128:(i + 1) * 128] for i in range(2)], bank, ['pb'])
            sc.op('dve', [('psmm', bank)], ['pT'], lambda e: e.tensor_copy(
                out=pT[:].rearrange("p k t -> p (k t)"), in_=pv[:, 0:256]))
            yield
            for cg in range(2):
                bank = mm_bank()
                sc.pe_group(['mT', 'Wo'], [('psmm', bank)], [
                    (lambda e, kc=kc: e.matmul(ps_mm[:, bank, :], lhsT=mT[:, kc, :], rhs=Wo[:, kc, cg * 512:(cg + 1) * 512],
                                               start=(kc == 0), stop=(kc == 7))) for kc in range(8)])
                sc.op('dve', [('psmm', bank), ('xr', rslot)], ['x1b'], lambda e, bank=bank, cg=cg: e.tensor_tensor(
                    out=x1b[:, cg * 512:(cg + 1) * 512], in0=ps_mm[:, bank, :], in1=xrs[:, cg * 512:(cg + 1) * 512], op=ALU.add))
                sc.op('dve', [('psmm', bank), ('xr', rslot)], [('xr', rslot)], lambda e, bank=bank, cg=cg: e.tensor_tensor(
                    out=xrs[:, cg * 512:(cg + 1) * 512], in0=ps_mm[:, bank, :], in1=xrs[:, cg * 512:(cg + 1) * 512], op=ALU.add))
                yield
            bank = mm_bank()
            pv = transposes([x1b[:, i * 128:(i + 1) * 128] for i in range(8)], bank, ['x1b'])
            sc.op('dve', [('psmm', bank)], ['mT'], lambda e: e.tensor_copy(
                out=mT[:].rearrange("p k t -> p (k t)"), in_=pv[:, 0:1024]))
            yield
            for cg in range(2):
                sgs = sq[:, 512:1024]
                bg = mm_bank()
                sc.pe_group(['mT', 'Wgt'], [('psmm', bg)], [
                    (lambda e, kc=kc: e.matmul(ps_mm[:, bg, :], lhsT=mT[:, kc, :], rhs=Wgt[:, kc, cg * 512:(cg + 1) * 512],
                                               start=(kc == 0), stop=(kc == 7))) for kc in range(8)])
                sc.op('act', [('psmm', bg)], ['sqhi'], lambda e, bg=bg, sgs=sgs: e.activation(
                    out=sgs, in_=ps_mm[:, bg, :], func=AF.Exp, scale=-1.0))
                sc.op('act', ['sqhi'], ['sqhi'], lambda e, sgs=sgs: e.activation(out=sgs, in_=sgs, func=AF.Ln, bias=1.0))
                sc.op('act', ['sqhi'], ['sqhi'], lambda e, sgs=sgs: e.activation(out=sgs, in_=sgs, func=AF.Exp, scale=-1.0))
                bp = mm_bank()
                sc.pe_group(['pT', 'Wp'], [('psmm', bp)], [
                    (lambda e, kc=kc: e.matmul(ps_mm[:, bp, :], lhsT=pT[:, kc, :], rhs=Wp[:, kc, cg * 512:(cg + 1) * 512],
                                               start=(kc == 0), stop=(kc == 1))) for kc in range(2)])
                sc.op('dve', [('psmm', bp), 'sqhi'], ['sqhi'], lambda e, bp=bp, sgs=sgs: e.tensor_tensor(
                    out=sgs, in0=ps_mm[:, bp, :], in1=sgs, op=ALU.mult))
                sc.op('dve', ['sqhi', ('xr', rslot)], [('xr', rslot)], lambda e, cg=cg, sgs=sgs: e.tensor_tensor(
                    out=xrs[:, cg * 512:(cg + 1) * 512], in0=xrs[:, cg * 512:(cg + 1) * 512], in1=sgs, op=ALU.add))
                yield
            sc.dma('sp', 'y%d' % rslot, [('xr', rslot)], [], ya_t, xrs)
            yield

        LA = 4

        def run(gens):
            gens = list(gens)
            while gens:
                for gen in list(gens):
                    try:
                        next(gen)
                    except StopIteration:
                        gens.remove(gen)

        def gen_late():
            wi = 0
            for (wd, wsb, nk, rn) in [(w_out, Wo, 8, 'Wo'), (w_gate, Wgt, 8, 'Wgt'), (w_ple, Wp, 2, 'Wp')]:
                for kc in range(nk):
                    i = wi % 2
                    wi += 1
                    sc.dma('sp', 'xr%d' % i, [], [('xr', i)], xr[:, i, :], wd[kc * 128:(kc + 1) * 128, :])
                    sc.op('pool', [('xr', i)], [rn], lambda e, wsb=wsb, kc=kc, i=i: e.tensor_copy(
                        out=wsb[:, kc, :], in_=xr[:, i, :]))
                    yield

        late = gen_late()
        late_alive = [True]

        def late_step():
            if late_alive[0]:
                try:
                    next(late)
                except StopIteration:
                    late_alive[0] = False

        load_x(0)
        for g in range(min(LA, NG)):
            gp = proj(g)
            while True:
                try:
                    next(gp)
                except StopIteration:
                    break
                late_step()
        while late_alive[0]:
            late_step()
        load_r(0)
        if NG > 1:
            load_r(1)
        for g in range(NG + 1):
            gens = []
            if g < NG:
                gens.append(units(g))
            t = g % NT
            deferred = None
            if g < NG and g + LA < NG:
                gens.append(proj(g + LA))
            if g - 1 >= 0:
                gens.append(tail(g - 1))
            run(gens)
            if deferred is not None:
                run([deferred])
            if g - 1 >= 0 and g + 1 < NG:
                load_r(g + 1)
        for ch in ('y0', 'y1'):
            if ch in sc.sem:
                sc.eng['sp'].wait_ge(sc.sem[ch], sc.cnt[ch])
    return nc


def _consts():
    ident = np.eye(128, dtype=np.float32)
    j64 = np.eye(64, dtype=np.float32)[::-1]
    j2 = np.zeros((128, 128), np.float32)
    j2[0:64, 0:64] = j64
    j2[64:128, 64:128] = j64
    kc = np.arange(64)[:, None]
    qc = np.arange(64)[None, :]
    cs = np.clip(qc - 8, 0, 48)
    cv = ((kc >= cs) & (kc < cs + 16)).astype(np.float32)
    cmask = np.tile(cv, (2, 2)).astype(np.float32)
    k = np.arange(128)[:, None]
    q = np.arange(128)[None, :]
    maska = np.stack([(k >= q), (k <= q)], axis=1).astype(np.float32)
    inv_freq = np.power(np.float32(500000.0), -np.arange(0, 16, 2, dtype=np.float32) / np.float32(16)).astype(np.float32)
    ang = (np.arange(S, dtype=np.float32)[:, None] * inv_freq[None, :]).astype(np.float32)
    cos = np.cos(ang).astype(np.float32)
    sin = np.sin(ang).astype(np.float32)
    ropec = np.concatenate([cos, cos], axis=1).astype(np.float32).reshape(NT, 128, 16).transpose(1, 0, 2).reshape(128, NT * 16)
    ropes = np.concatenate([-sin, sin], axis=1).astype(np.float32).reshape(NT, 128, 16).transpose(1, 0, 2).reshape(128, NT * 16)
    return dict(c_ident=ident, c_j2=j2, c_cmask=cmask, c_maska=np.ascontiguousarray(maska),
                c_ropec=np.ascontiguousarray(ropec), c_ropes=np.ascontiguousarray(ropes))


def _weights(norm_w, w_in, q_norm_a, k_norm_a, sink_a, q_norm_b, k_norm_b, rpb_b, w_out, w_ple, w_ple_gate):
    f = lambda a: np.ascontiguousarray(np.asarray(a, dtype=np.float32))
    return dict(norm_w=f(norm_w[0]), w_in=f(w_in[0]), q_norm_a=f(q_norm_a[0]), k_norm_a=f(k_norm_a[0]),
                sink_a=f(sink_a[0]), q_norm_b=f(q_norm_b[0]), k_norm_b=f(k_norm_b[0]),
                rpb_b=f(np.asarray(rpb_b[0]).reshape(120, 31)), w_out=f(w_out[0]), w_ple=f(w_ple[0]),
                w_gate=f(w_ple_gate[0]))


def kernel(x_prompt, x_sample, p_prompt, p_sample, norm_w, w_in, q_norm_a, k_norm_a, sink_a,
           q_norm_b, k_norm_b, rpb_b, w_out, w_ple, w_ple_gate):
    n = 8
    x_prompt = np.asarray(x_prompt, dtype=np.float32)
    x_sample = np.asarray(x_sample, dtype=np.float32)
    p_prompt = np.asarray(p_prompt, dtype=np.float32)[0]
    p_sample = np.asarray(p_sample, dtype=np.float32)[0]
    shared = _weights(norm_w, w_in, q_norm_a, k_norm_a, sink_a, q_norm_b, k_norm_b, rpb_b, w_out, w_ple, w_ple_gate)
    shared.update(_consts())
    nc = build(4, 1)
    in_maps = []
    for c in range(n):
        m = dict(shared)
        m["xp"] = np.ascontiguousarray(x_prompt[4 * c:4 * c + 4])
        m["pp"] = np.ascontiguousarray(p_prompt[4 * c:4 * c + 4])
        m["xs"] = np.ascontiguousarray(x_sample[c:c + 1])
        m["ps"] = np.ascontiguousarray(p_sample[c:c + 1])
        in_maps.append(m)
    res = run_bass_kernel_spmd(nc, in_maps, core_ids=list(range(n)))
    y_prompt = np.concatenate([np.asarray(r["yp"], dtype=np.float32) for r in res.results], axis=0)
    y_sample = np.concatenate([np.asarray(r["ys"], dtype=np.float32) for r in res.results], axis=0)
    return (y_prompt, y_sample)
```

```python
import numpy as np
from contextlib import ExitStack
import concourse.bass as bass
import concourse.mybir as mybir
from concourse.bass_utils import run_bass_kernel_spmd

F32 = mybir.dt.float32
BF16 = mybir.dt.bfloat16
AF = mybir.ActivationFunctionType
ALU = mybir.AluOpType
AX = mybir.AxisListType

D = 1024
S = 2048
NT = 16
INW = 3328
EPS = 1e-6
KR = 8
QR = 5
TRANSITIVE = True

WMAP = [(0, 0, 640), (640, 1280, 1024), (1664, 640, 128), (1792, 2304, 512),
        (2304, 768, 512), (2816, 2816, 512)]


def _b_tiles(t):
    out = []
    for u in range(NT):
        pat = []
        for kp in range(2):
            for qp in range(2):
                r = 2 * t + qp
                rs = min(max(r - 4, 0), 24)
                kr = 2 * u + kp
                pat.append(rs <= kr <= rs + 7)
        if any(pat):
            out.append((u, (u - t, tuple(pat))))
    return out


def _variants():
    vs = []
    for t in range(NT):
        for _, v in _b_tiles(t):
            if v not in vs:
                vs.append(v)
    return vs


VARIANTS = _variants()
NV = len(VARIANTS)


class Sched:
    def __init__(self, nc, es):
        self.nc = nc
        self.es = es
        self.eng = {'pe': nc.tensor, 'act': nc.scalar, 'dve': nc.vector, 'pool': nc.gpsimd, 'sp': nc.sync}
        self.sem = {}
        self.cnt = {}
        self.isdma = {}
        self.known = {e: {} for e in self.eng}
        self.res = {}
        self.pending = None
        self.snap = {}
        self.nstand = 0
        self.transitive = TRANSITIVE
        for e in self.eng:
            if e != 'sp':
                self.add_sem(e, False)

    def add_sem(self, name, isdma):
        self.sem[name] = self.es.enter_context(self.nc.semaphore("s_" + name))
        self.cnt[name] = 0
        self.isdma[name] = isdma

    def _need(self, e, ev):
        if ev is None:
            return
        src, v = ev
        if e == 'pe' and src == 'pe':
            return
        if self.isdma[src]:
            v = self.cnt[src]
        if self.known[e].get(src, 0) >= v:
            return
        if self.pending is not None:
            self.pending.append((src, v))
        else:
            self.eng[e].wait_ge(self.sem[src], v)
            self.nstand += 1
        self.known[e][src] = v
        if self.transitive:
            sn = self.snap.get((src, v))
            if sn:
                ke = self.known[e]
                for k2, v2 in sn.items():
                    if ke.get(k2, 0) < v2:
                        ke[k2] = v2

    def _deps(self, e, reads, writes):
        for r in reads:
            st = self.res.get(r)
            if st:
                self._need(e, st[0])
        for w in writes:
            st = self.res.get(w)
            if st:
                self._need(e, st[0])
                for src, v in list(st[1].items()):
                    self._need(e, (src, v))

    def _record(self, ev, reads, writes):
        src, v = ev
        for r in reads:
            st = self.res.setdefault(r, [None, {}])
            if st[1].get(src, 0) < v:
                st[1][src] = v
        for w in writes:
            self.res[w] = [ev, {}]

    def _collect(self, e, reads, writes):
        self.pending = []
        self._deps(e, reads, writes)
        waits, self.pending = self.pending, None
        last = {}
        for src, v in waits:
            last[src] = max(last.get(src, 0), v)
        waits = list(last.items())
        for src, v in waits[:-1]:
            self.eng[e].wait_ge(self.sem[src], v)
            self.nstand += 1
        return waits[-1] if waits else None

    def op(self, e, reads, writes, fn):
        w = self._collect(e, reads, writes)
        inst = fn(self.eng[e])
        if w is not None:
            inst._wait_ge(self.sem[w[0]], w[1])
        self.cnt[e] += 1
        inst.then_inc(self.sem[e], 1)
        self.snap[(e, self.cnt[e])] = dict(self.known[e])
        self._record((e, self.cnt[e]), reads, writes)

    def pe_group(self, reads, writes, fns):
        w = self._collect('pe', reads, writes)
        inst = None
        for i, fn in enumerate(fns):
            inst = fn(self.eng['pe'])
            if i == 0 and w is not None:
                inst._wait_ge(self.sem[w[0]], w[1])
        self.cnt['pe'] += 1
        inst.then_inc(self.sem['pe'], 1)
        self.snap[('pe', self.cnt['pe'])] = dict(self.known['pe'])
        self._record(('pe', self.cnt['pe']), reads, writes)

    def dma(self, q, chan, reads, writes, out, in_, serialize=True, **kw):
        if chan not in self.sem:
            self.add_sem(chan, True)
        if serialize and self.cnt[chan] > 0:
            self._need(q, (chan, self.cnt[chan]))
        self._deps(q, reads, writes)
        inst = self.eng[q].dma_start(out=out, in_=in_, **kw)
        self.cnt[chan] += 16
        inst.then_inc(self.sem[chan], 16)
        self._record((chan, self.cnt[chan]), reads, writes)

    def barrier(self, engines=None):
        for e in (engines or self.eng):
            for src in self.sem:
                if self.cnt[src] > 0:
                    self._need(e, (src, self.cnt[src]))


def build(nseq_p=4, nseq_s=1):
    nc = bass.Bass("TRN2", target_bir_lowering=False)
    nseq = nseq_p + nseq_s

    def din(name, shape):
        return nc.dram_tensor(name, list(shape), F32, kind="ExternalInput")

    xp = din("xp", [max(nseq_p, 1), S, D]).ap()
    pp = din("pp", [max(nseq_p, 1), S, 256]).ap()
    yp = nc.dram_tensor("yp", [max(nseq_p, 1), S, D], F32, kind="ExternalOutput").ap()
    if nseq_s:
        xs = din("xs", [nseq_s, S, D]).ap()
        pps = din("ps", [nseq_s, S, 256]).ap()
        ys = nc.dram_tensor("ys", [nseq_s, S, D], F32, kind="ExternalOutput").ap()
    norm_w = din("norm_w", [D]).ap()
    w_in = din("w_in", [D, INW]).ap()
    qna = din("q_norm_a", [64])
    kna = din("k_norm_a", [64])
    sink = din("sink_a", [8])
    qnb = din("q_norm_b", [64])
    knb = din("k_norm_b", [64])
    rpb = din("rpb_b", [120, 31]).ap()
    w_out = din("w_out", [D, D]).ap()
    w_ple = din("w_ple", [256, D]).ap()
    w_gate = din("w_gate", [D, D]).ap()
    c_ident = din("c_ident", [128, 128]).ap()
    c_j2 = din("c_j2", [128, 128]).ap()
    c_cmask = din("c_cmask", [128, 128]).ap()
    c_maska = din("c_maska", [128, 2, 128]).ap()
    c_ropec = din("c_ropec", [128, NT * 16]).ap()
    c_ropes = din("c_ropes", [128, NT * 16]).ap()
    pd = nc.dram_tensor("pd_scratch", [120, 128], F32, kind="Internal")

    def seq_aps(s):
        if s < nseq_p:
            return xp[s], pp[s], yp[s]
        return xs[s - nseq_p], pps[s - nseq_p], ys[s - nseq_p]

    with ExitStack() as es:
        sc = Sched(nc, es)

        def sb(name, shape, dt, stack=es):
            return stack.enter_context(nc.sbuf_tensor(name, list(shape), dt))

        Wg = sb("Wg", [128, 8, INW], BF16)
        Wo = sb("Wo", [128, 8, D], BF16)
        Wgt = sb("Wgt", [128, 8, D], BF16)
        Wp = sb("Wp", [128, 2, D], BF16)
        EB = sb("EB", [128, NV, 8, 128], BF16)
        idb = sb("idb", [128, 128], BF16)
        maskA = sb("maskA", [128, 2, 128], BF16)
        ropeC = sb("ropeC", [128, NT, 16], F32)
        ropeS = sb("ropeS", [128, NT, 16], F32)
        gpp = sb("gpp", [128, 4], F32)
        gain16 = sb("gain16", [128, 10, 16], F32)
        esink = sb("esink", [128, 8], F32)
        normw = sb("normw", [128, 8], F32)
        Qz = sb("Qz", [128, QR, 16, 128], BF16)
        Kt = sb("Kt", [128, KR, 5, 128], BF16)
        Vr = sb("Vr", [128, KR, 10, 65], BF16)
        Gr = sb("Gr", [128, QR, D], BF16)
        ps_mm = es.enter_context(nc.psum_tensor("ps_mm", [128, 2, 512], F32))
        ps_s = es.enter_context(nc.psum_tensor("ps_s", [128, 2, 1024], F32))
        ps_o = es.enter_context(nc.psum_tensor("ps_o", [128, 2, 512], F32))

        mmc = [0]

        def mm_bank():
            b = mmc[0] % 2
            mmc[0] += 1
            return b

        with ExitStack() as ies:
            wst2 = sb("wst2", [128, 2, INW], F32, ies)
            z = sb("z", [120, 128], F32, ies)
            Gt4 = sb("Gt4", [128, 3, 8, 128], F32, ies)
            ebf2 = sb("ebf2", [128, 1, 8, 128], F32, ies)
            j2 = sb("j2", [128, 128], F32, ies)
            cmask = sb("cmask", [128, 128], F32, ies)
            idf = sb("idf", [128, 128], F32, ies)
            mAf = sb("mAf", [128, 2, 128], F32, ies)
            g16 = sb("g16", [128, 2, 16], F32, ies)
            nw8 = sb("nw8", [8, 128], F32, ies)

            sc.op('pool', [], ['z'], lambda e: e.memset(z[:], 0.0))
            sc.op('pool', [], ['gpp'], lambda e: e.memset(gpp[:], 1.0))

            def mdma(chan, writes, out, in_, **kw):
                sc.dma('sp', chan, [], writes, out, in_, serialize=False, **kw)

            mdma('misc', ['z'], z[:, 48:79], rpb)
            mdma('misc', ['nw8'], nw8[:], norm_w.rearrange("(k p) -> k p", p=128))
            mdma('misc', ['idf'], idf[:], c_ident)
            mdma('misc', ['j2'], j2[:], c_j2)
            mdma('misc', ['cmask'], cmask[:], c_cmask)
            sc.dma('sp', 'pdw', ['z'], ['pd'], pd.ap(), z[:])
            sc.op('pool', [], ['Qz'], lambda e: e.memset(Qz[:], 0.0))
            sc.op('pool', [], ['Vr'], lambda e: e.memset(Vr[:], 1.0))
            sc.pe_group(['nw8', 'idf'], [('psmm', 0)], [
                lambda e: e.matmul(ps_mm[:, 0, 0:8], lhsT=nw8[0:8, :], rhs=idf[0:8, 0:8], start=True, stop=True)])
            sc.op('dve', [('psmm', 0)], ['normw'], lambda e: e.tensor_copy(out=normw[:], in_=ps_mm[:, 0, 0:8]))
            sc.op('dve', ['idf'], ['idb'], lambda e: e.tensor_copy(out=idb[:], in_=idf[:]))

            gn = sb("gn", [4, 128], F32, ies)

            def emit_misc2():
                mdma('misc2', ['mAf'], mAf[:], c_maska)
                mdma('misc2', ['ropeC'], ropeC[:].rearrange("p t c -> p (t c)"), c_ropec)
                mdma('misc2', ['ropeS'], ropeS[:].rearrange("p t c -> p (t c)"), c_ropes)
                mdma('misc2', ['esink'], esink[:], bass.AP(sink, 0, [[0, 128], [1, 8]]))
                mdma('misc2', ['g16'], g16[:, 0, :], bass.AP(qna, 0, [[0, 128], [1, 16]]))
                mdma('misc2', ['g16'], g16[:, 1, :], bass.AP(kna, 0, [[0, 128], [1, 16]]))
                for row, src in enumerate([qna, kna, qnb, knb]):
                    for half in range(2):
                        mdma('misc2', ['gn'], gn[row:row + 1, half * 64:(half + 1) * 64], bass.AP(src, 0, [[0, 1], [1, 64]]))

            def gen_w():
                wi = 0
                for kc in range(8):
                    i = wi % 2
                    wi += 1
                    wst = wst2[:, i, :]
                    sc.dma('sp', 'wst%d' % i, [], [('wst', i)], wst, w_in[kc * 128:(kc + 1) * 128, :])
                    for (dst, src, w) in WMAP:
                        if dst < 1664:
                            sc.op('dve', [('wst', i), 'normw'], ['W'], lambda e, dst=dst, src=src, w=w, kc=kc, wst=wst: e.tensor_scalar(
                                out=Wg[:, kc, dst:dst + w], in0=wst[:, src:src + w], scalar1=normw[:, kc:kc + 1],
                                scalar2=None, op0=ALU.mult))
                        else:
                            sc.op('act', [('wst', i), 'normw'], ['W'], lambda e, dst=dst, src=src, w=w, kc=kc, wst=wst: e.activation(
                                out=Wg[:, kc, dst:dst + w], in_=wst[:, src:src + w], func=AF.Copy, scale=normw[:, kc:kc + 1]))
                    if kc == 1:
                        emit_misc2()
                    yield

            def eb_dma(vi):
                delta, pat = VARIANTS[vi]
                gb = vi % 3
                for qp in range(2):
                    for kp in range(2):
                        drow = min(max(2 * delta + 7 + kp - qp, 0), 14)
                        sc.dma('act', 'gt%d' % gb, ['pd'], [('Gt', gb)],
                               Gt4[qp * 64:(qp + 1) * 64, gb, :, kp * 64:(kp + 1) * 64],
                               bass.AP(pd, drow * 128, [[1, 64], [15 * 128, 8], [1, 64]]), serialize=False)

            def gen_eb():
                for vi, (delta, pat) in enumerate(VARIANTS):
                    bb = vi % 2
                    gb = vi % 3
                    Gt = Gt4[:, gb]
                    ebf = ebf2[:, 0]
                    for half in range(2):
                        sc.pe_group([('Gt', gb), 'j2'], [('pss', bb)], [
                            (lambda e, h=h: e.matmul(ps_s[:, bb, h * 128:(h + 1) * 128], lhsT=Gt[:, h, :], rhs=j2[:],
                                                     start=True, stop=True)) for h in range(half * 4, half * 4 + 4)])
                    sc.op('act', [('pss', bb)], [('ebf', 0)], lambda e: e.activation(
                        out=ebf.rearrange("p h q -> p (h q)"), in_=ps_s[:, bb, :], func=AF.Exp))
                    sc.op('dve', [('ebf', 0), 'cmask'], ['EB'], lambda e, vi=vi: e.tensor_tensor(
                        out=EB[:, vi, :, :], in0=ebf, in1=cmask[:].unsqueeze(1).to_broadcast([128, 8, 128]), op=ALU.mult))
                    for kp in range(2):
                        for qp in range(2):
                            if not pat[kp * 2 + qp]:
                                sc.op('pool', [], ['EB'], lambda e, vi=vi, kp=kp, qp=qp: e.memset(
                                    EB[kp * 64:(kp + 1) * 64, vi, :, qp * 64:(qp + 1) * 64], 0.0))
                    if vi + 3 < NV:
                        eb_dma(vi + 3)
                    yield

            for vi in range(min(3, NV)):
                eb_dma(vi)
            gens = [gen_w(), gen_eb()]
            while gens:
                for gen in list(gens):
                    try:
                        next(gen)
                    except StopIteration:
                        gens.remove(gen)
            sc.pe_group(['gn', 'idf'], [('psmm', 1)], [
                lambda e: e.matmul(ps_mm[:, 1, 0:4], lhsT=gn[0:4, :], rhs=idf[0:4, 0:4], start=True, stop=True)])
            sc.op('dve', [('psmm', 1)], ['gpp'], lambda e: e.tensor_copy(out=gpp[:], in_=ps_mm[:, 1, 0:4]))
            for half in range(2):
                sc.op('dve', ['gpp'], ['gpp'], lambda e, half=half: e.memset(gpp[half * 64:half * 64 + 16, 0:2], 1.0))
            sc.op('act', ['esink'], ['esink'], lambda e: e.activation(out=esink[:], in_=esink[:], func=AF.Exp))
            sc.op('dve', ['mAf'], ['maskA'], lambda e: e.tensor_copy(out=maskA[:], in_=mAf[:]))
            sc.op('dve', ['g16'], ['gain16'], lambda e: e.tensor_copy(
                out=gain16[:, 0:8, :], in_=g16[:, 0:1, :].to_broadcast([128, 8, 16])))
            sc.op('dve', ['g16'], ['gain16'], lambda e: e.tensor_copy(
                out=gain16[:, 8:10, :], in_=g16[:, 1:2, :].to_broadcast([128, 2, 16])))
            sc.barrier()
        xt = sb("xt", [128, 1, D], F32)
        xT = sb("xT", [128, 8, 128], BF16)
        uA = sb("uA", [128, 640], F32)
        uB = sb("uB", [128, 1024], F32)
        sq = sb("sq", [128, 1024], F32)
        qn2A = sb("qn2A", [128, 640], BF16)
        qn2B = sb("qn2B", [128, 1024], BF16)
        xb = qn2B
        otmp = sb("otmp", [128, 512], F32)
        r0 = sb("r0", [128, 10, 16], F32)
        r1 = sb("r1", [128, 10, 16], F32)
        r2 = sb("r2", [128, 10, 16], F32)
        stx = sb("stx", [128, 8], F32)
        sth = sb("sth", [128, 3, 16], F32)
        den = sb("den", [128, 2, 8], F32)
        PT = sb("PT", [128, 3, 1024], BF16)
        x1b = sb("x1b", [128, D], BF16)
        mix = sb("mix", [128, D], BF16)
        mT = sb("mT", [128, 8, 128], BF16)
        xr = sb("xr", [128, 2, D], F32)
        pt = sb("pt", [128, 2, 256], F32)
        pb = sb("pb", [128, 256], BF16)
        pT = sb("pT", [128, 2, 128], BF16)

        def psT(bank):
            return ps_mm[:, bank, :].bitcast(BF16)

        def transposes(src_blocks, bank, reads):
            pv = psT(bank)
            sc.pe_group(reads + ['idb'], [('psmm', bank)], [
                (lambda e, i=i, blk=blk: e.transpose(out=pv[:, i * 128:(i + 1) * 128], in_=blk, identity=idb[:]))
                for i, blk in enumerate(src_blocks)])
            return pv

        NG = nseq * NT

        def g_aps(g):
            xa, pa, ya = seq_aps(g // NT)
            t = g % NT
            return xa[t * 128:(t + 1) * 128, :], pa[t * 128:(t + 1) * 128, :], ya[t * 128:(t + 1) * 128, :], t

        def load_x(g):
            xa, _, _, _ = g_aps(g)
            sc.dma('sp', 'xt0', [], [('xt', 0)], xt[:, 0, :], xa)

        def load_r(g):
            xa, _, _, _ = g_aps(g)
            slot = g % 2
            sc.dma('sp', 'xr%d' % slot, [], [('xr', slot)], xr[:, slot, :], xa)

        def proj(g):
            _, _, _, j = g_aps(g)
            qs, ks = g % QR, g % KR
            xs_ = xt[:, 0, :]
            sc.op('act', [('xt', 0)], ['sq', 'sqhi', 'stx'], lambda e: e.activation(
                out=sq[:], in_=xs_, func=AF.Square, accum_out=stx[:, 0:1]))
            sc.op('dve', [('xt', 0)], ['qn2B'], lambda e: e.tensor_copy(out=xb[:], in_=xs_))
            if g + 1 < NG:
                load_x(g + 1)
            sc.op('dve', ['stx'], ['stx1'], lambda e: e.tensor_scalar(
                out=stx[:, 1:2], in0=stx[:, 0:1], scalar1=1.0 / D, scalar2=EPS, op0=ALU.mult, op1=ALU.add))
            sc.op('act', ['stx1'], ['stx2'], lambda e: e.activation(out=stx[:, 2:3], in_=stx[:, 1:2], func=AF.Ln))
            sc.op('act', ['stx2'], ['stx3'], lambda e: e.activation(out=stx[:, 3:4], in_=stx[:, 2:3], func=AF.Exp, scale=-0.5))
            sc.op('dve', ['stx1'], ['stx4'], lambda e: e.tensor_scalar(
                out=stx[:, 4:5], in0=stx[:, 1:2], scalar1=EPS, scalar2=None, op0=ALU.mult))
            sc.op('dve', ['stx3'], ['stx5'], lambda e: e.tensor_scalar(
                out=stx[:, 5:6], in0=stx[:, 3:4], scalar1=-1.0, scalar2=None, op0=ALU.mult))
            rstd = stx[:, 3:4]
            nrstd = stx[:, 5:6]
            yield
            bank = mm_bank()
            pv = transposes([xb[:, i * 128:(i + 1) * 128] for i in range(8)], bank, ['qn2B'])
            sc.op('dve', [('psmm', bank)], ['xT'], lambda e: e.tensor_copy(
                out=xT[:].rearrange("p k t -> p (k t)"), in_=pv[:, 0:1024]))
            yield

            def group(c0, w):
                bank = mm_bank()
                sc.pe_group(['xT', 'W'], [('psmm', bank)], [
                    (lambda e, kc=kc: e.matmul(ps_mm[:, bank, 0:w], lhsT=xT[:, kc, :], rhs=Wg[:, kc, c0:c0 + w],
                                               start=(kc == 0), stop=(kc == 7))) for kc in range(8)])
                return bank

            def qknorm(ubuf, uname, nh, eps_ap):
                w = nh * 64
                sc.op('act', [uname], ['sq', 'sqhi'], lambda e: e.activation(out=sq[:, 0:w], in_=ubuf[:, 0:w], func=AF.Square))
                sc.op('dve', ['sq', 'sqhi'], ['sth0'], lambda e: e.reduce_sum(
                    out=sth[:, 0, 0:nh], in_=sq[:, 0:w].rearrange("p (h d) -> p h d", d=64), axis=AX.X))
                sc.op('dve', ['sth0', 'stx4'], ['sth0'], lambda e: e.tensor_scalar(
                    out=sth[:, 0, 0:nh], in0=sth[:, 0, 0:nh], scalar1=1.0 / 64, scalar2=eps_ap, op0=ALU.mult, op1=ALU.add))
                sc.op('act', ['sth0'], ['sth1'], lambda e: e.activation(out=sth[:, 1, 0:nh], in_=sth[:, 0, 0:nh], func=AF.Ln))
                sc.op('act', ['sth1'], ['sth2'], lambda e: e.activation(
                    out=sth[:, 2, 0:nh], in_=sth[:, 1, 0:nh], func=AF.Exp, scale=-0.5))

            b0 = group(0, 512)
            sc.op('act', [('psmm', b0)], ['uA'], lambda e: e.activation(out=uA[:, 0:512], in_=ps_mm[:, b0, :], func=AF.Copy))
            yield
            b1 = group(512, 128)
            sc.op('act', [('psmm', b1)], ['uA'], lambda e: e.activation(out=uA[:, 512:640], in_=ps_mm[:, b1, 0:128], func=AF.Copy))
            qknorm(uA, 'uA', 10, stx[:, 4:5])
            u3 = uA[:, 0:640].rearrange("p (h d) -> p h d", d=64)
            sc.op('dve', ['uA', 'sth2'], ['qn2A'], lambda e: e.tensor_tensor(
                out=qn2A[:, 0:512].rearrange("p (g kv d) -> p kv g d", kv=2, d=64),
                in0=uA[:, 0:512].rearrange("p (kv g d) -> p kv g d", kv=2, d=64),
                in1=sth[:, 2, 0:8].rearrange("p (kv g) -> p kv g", kv=2).unsqueeze(3).to_broadcast([128, 2, 4, 64]),
                op=ALU.mult))
            sc.op('dve', ['uA', 'sth2'], ['qn2A'], lambda e: e.tensor_tensor(
                out=qn2A[:, 512:640].rearrange("p (h d) -> p h d", d=64), in0=u3[:, 8:10, :],
                in1=sth[:, 2, 8:10].unsqueeze(2).to_broadcast([128, 2, 64]), op=ALU.mult))
            sc.op('dve', ['uA', 'sth2'], ['r0'], lambda e: e.tensor_tensor(
                out=r0[:], in0=u3[:, :, 0:16], in1=sth[:, 2, 0:10].unsqueeze(2).to_broadcast([128, 10, 16]), op=ALU.mult))
            sc.op('pool', ['r0', 'gain16'], ['r0'], lambda e: e.tensor_tensor(out=r0[:], in0=r0[:], in1=gain16[:], op=ALU.mult))
            sc.op('pool', ['r0', 'ropeC'], ['r1'], lambda e: e.tensor_tensor(
                out=r1[:], in0=r0[:], in1=ropeC[:, j:j + 1, :].to_broadcast([128, 10, 16]), op=ALU.mult))
            sc.op('pool', ['r0', 'ropeS'], ['r2'], lambda e: e.tensor_tensor(
                out=r2[:, :, 0:8], in0=r0[:, :, 8:16], in1=ropeS[:, j:j + 1, 0:8].to_broadcast([128, 10, 8]), op=ALU.mult))
            sc.op('pool', ['r0', 'ropeS'], ['r2'], lambda e: e.tensor_tensor(
                out=r2[:, :, 8:16], in0=r0[:, :, 0:8], in1=ropeS[:, j:j + 1, 8:16].to_broadcast([128, 10, 8]), op=ALU.mult))
            sc.op('pool', ['r1', 'r2'], ['qn2A'], lambda e: e.tensor_tensor(
                out=qn2A[:, 0:512].rearrange("p (g kv d) -> p kv g d", kv=2, d=64)[:, :, :, 0:16],
                in0=r1[:, 0:8, :].rearrange("p (kv g) d -> p kv g d", kv=2),
                in1=r2[:, 0:8, :].rearrange("p (kv g) d -> p kv g d", kv=2), op=ALU.add))
            sc.op('pool', ['r1', 'r2'], ['qn2A'], lambda e: e.tensor_tensor(
                out=qn2A[:, 512:640].rearrange("p (h d) -> p h d", d=64)[:, :, 0:16],
                in0=r1[:, 8:10, :], in1=r2[:, 8:10, :], op=ALU.add))
            yield
            for gi in range(2):
                bq = group(640 + gi * 512, 512)
                sc.op('act', [('psmm', bq)], ['uB'], lambda e, bq=bq, gi=gi: e.activation(
                    out=uB[:, gi * 512:(gi + 1) * 512], in_=ps_mm[:, bq, :], func=AF.Copy))
                if gi == 0:
                    yield
            qknorm(uB, 'uB', 16, stx[:, 4:5])
            sc.op('dve', ['uB', 'sth2'], ['qn2B'], lambda e: e.tensor_tensor(
                out=qn2B[:].rearrange("p (h d) -> p h d", d=64), in0=uB[:].rearrange("p (h d) -> p h d", d=64),
                in1=sth[:, 2, 0:16].unsqueeze(2).to_broadcast([128, 16, 64]), op=ALU.mult))
            yield
            bv = group(1664, 512)
            sc.op('act', [('psmm', bv), 'stx3'], [('V', ks)], lambda e: e.activation(
                out=Vr[:, ks, 0:8, 0:64], in_=ps_mm[:, bv, :].rearrange("p (h d) -> p h d", d=64),
                func=AF.Identity, scale=rstd))
            yield
            bv2 = group(2176, 128)
            sc.op('act', [('psmm', bv2), 'stx3'], [('V', ks)], lambda e: e.activation(
                out=Vr[:, ks, 8:10, 0:64], in_=ps_mm[:, bv2, 0:128].rearrange("p (h d) -> p h d", d=64),
                func=AF.Identity, scale=rstd))
            for gi in range(2):
                gtmp = sq[:, gi * 512:(gi + 1) * 512]
                zcb = uB[:, 0:512] if gi == 0 else uA[:, 0:512]
                zname = 'uB' if gi == 0 else 'uA'
                sqn = 'sq' if gi == 0 else 'sqhi'
                bg = group(2304 + gi * 512, 512)
                sc.op('act', [('psmm', bg), 'stx5'], [sqn], lambda e, bg=bg, gtmp=gtmp: e.activation(
                    out=gtmp, in_=ps_mm[:, bg, :], func=AF.Exp, scale=nrstd))
                sc.op('dve', [('psmm', bg), 'stx3', sqn], [zname], lambda e, bg=bg, zcb=zcb: e.tensor_scalar(
                    out=zcb, in0=ps_mm[:, bg, :], scalar1=rstd, scalar2=None, op0=ALU.mult))
                sc.op('act', [sqn], [sqn], lambda e, gtmp=gtmp: e.activation(out=gtmp, in_=gtmp, func=AF.Ln, bias=1.0))
                sc.op('act', [sqn], [sqn], lambda e, gtmp=gtmp: e.activation(out=gtmp, in_=gtmp, func=AF.Exp, scale=-1.0))
                sc.op('pool', [zname, sqn], [('G', qs)], lambda e, gi=gi, gtmp=gtmp, zcb=zcb: e.tensor_tensor(
                    out=Gr[:, qs, gi * 512:(gi + 1) * 512], in0=zcb, in1=gtmp, op=ALU.mult))
                yield
            bank = mm_bank()
            pv = transposes([qn2A[:, i * 128:(i + 1) * 128] for i in range(5)], bank, ['qn2A'])
            pv3 = pv[:, 0:640].rearrange("p (b t) -> p b t", t=128)
            sc.op('dve', [('psmm', bank), 'gpp'], [('Qz', qs)], lambda e: e.tensor_scalar(
                out=Qz[0:64, qs, 0:4, :], in0=pv3[0:64, 0:4, :], scalar1=gpp[0:64, 0:1], scalar2=None, op0=ALU.mult))
            sc.op('dve', [('psmm', bank), 'gpp'], [('Qz', qs)], lambda e: e.tensor_scalar(
                out=Qz[64:128, qs, 4:8, :], in0=pv3[64:128, 0:4, :], scalar1=gpp[64:128, 0:1], scalar2=None, op0=ALU.mult))
            sc.op('dve', [('psmm', bank), 'gpp'], [('K', ks)], lambda e: e.tensor_scalar(
                out=Kt[:, ks, 0, :], in0=pv3[:, 4, :], scalar1=gpp[:, 1:2], scalar2=None, op0=ALU.mult))
            yield
            bank = mm_bank()
            pv = transposes([qn2B[:, i * 128:(i + 1) * 128] for i in range(8)], bank, ['qn2B'])
            pv3 = pv[:, 0:1024].rearrange("p (b t) -> p b t", t=128)
            qz_b = Qz[:, qs, 8:16, :].rearrange("p (i two) t -> p i two t", two=2)
            sc.op('act', [('psmm', bank), 'gpp'], [('Qz', qs)], lambda e: e.activation(
                out=qz_b[0:64, :, 0, :], in_=pv3[0:64, 0:4, :], func=AF.Copy, scale=gpp[0:64, 2:3]))
            sc.op('act', [('psmm', bank), 'gpp'], [('Qz', qs)], lambda e: e.activation(
                out=qz_b[64:128, :, 1, :], in_=pv3[64:128, 0:4, :], func=AF.Copy, scale=gpp[64:128, 2:3]))
            sc.op('act', [('psmm', bank), 'gpp'], [('K', ks)], lambda e: e.activation(
                out=Kt[:, ks, 1:5, :], in_=pv3[:, 4:8, :], func=AF.Copy, scale=gpp[:, 3:4]))
            yield

        sreg = [0]
        rot = [0]

        def o_finish(with_sink, qs, col0):
            o4 = ps_o[:, :, 0:260].rearrange("p b (h c) -> p b h c", c=65)
            d3 = den[:, 0, :].rearrange("p (b h) -> p b h", b=2)
            if with_sink:
                sc.op('dve', ['pso', 'esink'], ['den0'], lambda e: e.tensor_tensor(
                    out=d3, in0=o4[:, :, :, 64], in1=esink[:].rearrange("p (b h) -> p b h", b=2), op=ALU.add))
            else:
                sc.op('dve', ['pso'], ['den0'], lambda e: e.tensor_copy(out=d3, in_=o4[:, :, :, 64]))
            sc.op('dve', ['den0'], ['den1'], lambda e: e.reciprocal(out=den[:, 1, :], in_=den[:, 0, :]))
            r3 = den[:, 1, :].rearrange("p (b h) -> p b h", b=2)
            sc.op('dve', ['pso', 'den1'], ['otmp'], lambda e: e.tensor_tensor(
                out=otmp[:].rearrange("p (b h d) -> p b h d", b=2, d=64), in0=o4[:, :, :, 0:64],
                in1=r3.unsqueeze(3).to_broadcast([128, 2, 4, 64]), op=ALU.mult))
            sc.op('dve', ['otmp', ('G', qs)], ['mix'], lambda e: e.tensor_tensor(
                out=mix[:, col0:col0 + 512], in0=otmp[:], in1=Gr[:, qs, col0:col0 + 512], op=ALU.mult))

        def units(g):
            xa_t, pa_t, ya_t, t = g_aps(g)
            base = (g // NT) * NT
            qs = g % QR
            rslot = g % 2
            xrs = xr[:, rslot, :]
            sc.dma('sp', 'pt%d' % rslot, [], [('pt', rslot)], pt[:, rslot, :], pa_t)
            units = []
            for b in (t - 1, t, t + 1):
                if 0 <= b < NT:
                    units.append(('A', b, None if b == t else (0 if b < t else 1)))
            nA = len(units)
            for (uu, var) in _b_tiles(t):
                units.append(('B', uu, VARIANTS.index(var)))
            nU = len(units)
            regs = {}
            pts = {}

            def emit_S(i):
                kind, kt, _ = units[i]
                reg = sreg[0] % 2
                sreg[0] += 1
                regs[i] = reg
                ks = (base + kt) % KR
                if kind == 'A':
                    sc.pe_group([('K', ks), ('Qz', qs)], [('pss', reg)], [
                        (lambda e, kv=kv: e.matmul(ps_s[:, reg, kv * 512:(kv + 1) * 512], lhsT=Kt[:, ks, 0, :],
                                                   rhs=Qz[:, qs, 4 * kv:4 * kv + 4, :].rearrange("p h t -> p (h t)"),
                                                   start=True, stop=True)) for kv in range(2)])
                else:
                    for half in range(2):
                        sc.pe_group([('K', ks), ('Qz', qs)], [('pss', reg)], [
                            (lambda e, pi=pi: e.matmul(ps_s[:, reg, pi * 256:(pi + 1) * 256], lhsT=Kt[:, ks, 1 + pi, :],
                                                       rhs=Qz[:, qs, 8 + 2 * pi:8 + 2 * pi + 2, :].rearrange("p h t -> p (h t)"),
                                                       start=True, stop=True))
                            for pi in range(half * 2, half * 2 + 2)])

            def emit_E(i):
                kind, kt, aux = units[i]
                reg = regs[i]
                r = rot[0] % 3
                rot[0] += 1
                pts[i] = r
                if kind == 'A':
                    sc.op('act', [('pss', reg)], [('PT', r)], lambda e: e.activation(
                        out=PT[:, r, :], in_=ps_s[:, reg, :], func=AF.Exp, scale=0.125))
                    if aux is not None:
                        sc.op('dve', [('PT', r), 'maskA'], [('PT', r)], lambda e: e.tensor_tensor(
                            out=PT[:, r, :].rearrange("p (h q) -> p h q", q=128),
                            in0=PT[:, r, :].rearrange("p (h q) -> p h q", q=128),
                            in1=maskA[:, aux:aux + 1, :].to_broadcast([128, 8, 128]), op=ALU.mult))
                else:
                    sc.op('act', [('pss', reg)], [('PT', r)], lambda e: e.activation(
                        out=PT[:, r, :], in_=ps_s[:, reg, :], func=AF.Exp, scale=0.125))
                    sc.op('dve', [('PT', r), 'EB'], [('PT', r)], lambda e: e.tensor_tensor(
                        out=PT[:, r, :], in0=PT[:, r, :], in1=EB[:, aux, :, :].rearrange("p h q -> p (h q)"), op=ALU.mult))

            def emit_PV(i):
                kind, kt, _ = units[i]
                r = pts[i]
                ks = (base + kt) % KR
                if kind == 'A':
                    first, last = (i == 0), (i == nA - 1)
                    sc.pe_group([('PT', r), ('V', ks)], ['pso'], [
                        (lambda e, h=h: e.matmul(ps_o[:, h // 4, (h % 4) * 65:(h % 4) * 65 + 65],
                                                 lhsT=PT[:, r, h * 128:(h + 1) * 128], rhs=Vr[:, ks, h // 4, :],
                                                 start=(first and h % 4 == 0), stop=last,
                                                 skip_group_check=True)) for h in range(8)])
                else:
                    first, last = (i == nA), (i == nU - 1)
                    sc.pe_group([('PT', r), ('V', ks)], ['pso'], [
                        (lambda e, h=h: e.matmul(ps_o[:, h // 4, (h % 4) * 65:(h % 4) * 65 + 65],
                                                 lhsT=PT[:, r, h * 128:(h + 1) * 128], rhs=Vr[:, ks, 2 + h, :],
                                                 start=(first and h % 4 == 0), stop=last,
                                                 skip_group_check=True)) for h in range(8)])

            emit_S(0)
            yield
            for k in range(1, nU + 3):
                if k < nU:
                    emit_S(k)
                if k - 1 < nU:
                    emit_E(k - 1)
                if 0 <= k - 3:
                    emit_PV(k - 3)
                    if k - 3 == nA - 1:
                        o_finish(True, qs, 0)
                yield
            o_finish(False, qs, 512)
            yield

        def tail(g):
            xa_t, pa_t, ya_t, t = g_aps(g)
            base = (g // NT) * NT
            qs = g % QR
            rslot = g % 2
            xrs = xr[:, rslot, :]
            sc.op('pool', [('pt', rslot)], ['pb'], lambda e: e.tensor_copy(out=pb[:], in_=pt[:, rslot, :]))
            bank = mm_bank()
            pv = transposes([mix[:, i * 128:(i + 1) * 128] for i in range(8)], bank, ['mix'])
            sc.op('dve', [('psmm', bank)], ['mT'], lambda e: e.tensor_copy(
                out=mT[:].rearrange("p k t -> p (k t)"), in_=pv[:, 0:1024]))
            bank = mm_bank()
            pv = transposes([pb[:, i * 128:(i + 1) * 128] for i in range(2)], bank, ['pb'])
            sc.op('dve', [('psmm', bank)], ['pT'], lambda e: e.tensor_copy(
                out=pT[:].rearrange("p k t -> p (k t)"), in_=pv[:, 0:256]))
            yield
            for cg in range(2):
                bank = mm_bank()
                sc.pe_group(['mT', 'Wo'], [('psmm', bank)], [
                    (lambda e, kc=kc: e.matmul(ps_mm[:, bank, :], lhsT=mT[:, kc, :], rhs=Wo[:, kc, cg * 512:(cg + 1) * 512],
                                               start=(kc == 0), stop=(kc == 7))) for kc in range(8)])
                sc.op('dve', [('psmm', bank), ('xr', rslot)], ['x1b'], lambda e, bank=bank, cg=cg: e.tensor_tensor(
                    out=x1b[:, cg * 512:(cg + 1) * 512], in0=ps_mm[:, bank, :], in1=xrs[:, cg * 512:(cg + 1) * 512], op=ALU.add))
                sc.op('dve', [('psmm', bank), ('xr', rslot)], [('xr', rslot)], lambda e, bank=bank, cg=cg: e.tensor_tensor(
                    out=xrs[:, cg * 512:(cg + 1) * 512], in0=ps_mm[:, bank, :], in1=xrs[:, cg * 512:(cg + 1) * 512], op=ALU.add))
                yield
            bank = mm_bank()
            pv = transposes([x1b[:, i * 128:(i + 1) * 128] for i in range(8)], bank, ['x1b'])
            sc.op('dve', [('psmm', bank)], ['mT'], lambda e: e.tensor_copy(
                out=mT[:].rearrange("p k t -> p (k t)"), in_=pv[:, 0:1024]))
            yield
            for cg in range(2):
                sgs = sq[:, 512:1024]
                bg = mm_bank()
                sc.pe_group(['mT', 'Wgt'], [('psmm', bg)], [
                    (lambda e, kc=kc: e.matmul(ps_mm[:, bg, :], lhsT=mT[:, kc, :], rhs=Wgt[:, kc, cg * 512:(cg + 1) * 512],
                                               start=(kc == 0), stop=(kc == 7))) for kc in range(8)])
                sc.op('act', [('psmm', bg)], ['sqhi'], lambda e, bg=bg, sgs=sgs: e.activation(
                    out=sgs, in_=ps_mm[:, bg, :], func=AF.Exp, scale=-1.0))
                sc.op('act', ['sqhi'], ['sqhi'], lambda e, sgs=sgs: e.activation(out=sgs, in_=sgs, func=AF.Ln, bias=1.0))
                sc.op('act', ['sqhi'], ['sqhi'], lambda e, sgs=sgs: e.activation(out=sgs, in_=sgs, func=AF.Exp, scale=-1.0))
                bp = mm_bank()
                sc.pe_group(['pT', 'Wp'], [('psmm', bp)], [
                    (lambda e, kc=kc: e.matmul(ps_mm[:, bp, :], lhsT=pT[:, kc, :], rhs=Wp[:, kc, cg * 512:(cg + 1) * 512],
                                               start=(kc == 0), stop=(kc == 1))) for kc in range(2)])
                sc.op('dve', [('psmm', bp), 'sqhi'], ['sqhi'], lambda e, bp=bp, sgs=sgs: e.tensor_tensor(
                    out=sgs, in0=ps_mm[:, bp, :], in1=sgs, op=ALU.mult))
                sc.op('dve', ['sqhi', ('xr', rslot)], [('xr', rslot)], lambda e, cg=cg, sgs=sgs: e.tensor_tensor(
                    out=xrs[:, cg * 512:(cg + 1) * 512], in0=xrs[:, cg * 512:(cg + 1) * 512], in1=sgs, op=ALU.add))
                yield
            sc.dma('sp', 'y%d' % rslot, [('xr', rslot)], [], ya_t, xrs)
            yield

        LA = 4

        def run(gens):
            gens = list(gens)
            while gens:
                for gen in list(gens):
                    try:
                        next(gen)
                    except StopIteration:
                        gens.remove(gen)

        def gen_late():
            wi = 0
            for (wd, wsb, nk, rn) in [(w_out, Wo, 8, 'Wo'), (w_gate, Wgt, 8, 'Wgt'), (w_ple, Wp, 2, 'Wp')]:
                for kc in range(nk):
                    i = wi % 2
                    wi += 1
                    sc.dma('sp', 'xr%d' % i, [], [('xr', i)], xr[:, i, :], wd[kc * 128:(kc + 1) * 128, :])
                    sc.op('pool', [('xr', i)], [rn], lambda e, wsb=wsb, kc=kc, i=i: e.tensor_copy(
                        out=wsb[:, kc, :], in_=xr[:, i, :]))
                    yield

        late = gen_late()
        late_alive = [True]

        def late_step():
            if late_alive[0]:
                try:
                    next(late)
                except StopIteration:
                    late_alive[0] = False

        load_x(0)
        for g in range(min(LA, NG)):
            gp = proj(g)
            while True:
                try:
                    next(gp)
                except StopIteration:
                    break
                late_step()
        while late_alive[0]:
            late_step()
        load_r(0)
        if NG > 1:
            load_r(1)
        for g in range(NG + 1):
            gens = []
            if g < NG:
                gens.append(units(g))
            t = g % NT
            deferred = None
            if g < NG and g + LA < NG:
                gens.append(proj(g + LA))
            if g - 1 >= 0:
                gens.append(tail(g - 1))
            run(gens)
            if deferred is not None:
                run([deferred])
            if g - 1 >= 0 and g + 1 < NG:
                load_r(g + 1)
        for ch in ('y0', 'y1'):
            if ch in sc.sem:
                sc.eng['sp'].wait_ge(sc.sem[ch], sc.cnt[ch])
    return nc


def _consts():
    ident = np.eye(128, dtype=np.float32)
    j64 = np.eye(64, dtype=np.float32)[::-1]
    j2 = np.zeros((128, 128), np.float32)
    j2[0:64, 0:64] = j64
    j2[64:128, 64:128] = j64
    kc = np.arange(64)[:, None]
    qc = np.arange(64)[None, :]
    cs = np.clip(qc - 8, 0, 48)
    cv = ((kc >= cs) & (kc < cs + 16)).astype(np.float32)
    cmask = np.tile(cv, (2, 2)).astype(np.float32)
    k = np.arange(128)[:, None]
    q = np.arange(128)[None, :]
    maska = np.stack([(k >= q), (k <= q)], axis=1).astype(np.float32)
    inv_freq = np.power(np.float32(500000.0), -np.arange(0, 16, 2, dtype=np.float32) / np.float32(16)).astype(np.float32)
    ang = (np.arange(S, dtype=np.float32)[:, None] * inv_freq[None, :]).astype(np.float32)
    cos = np.cos(ang).astype(np.float32)
    sin = np.sin(ang).astype(np.float32)
    ropec = np.concatenate([cos, cos], axis=1).astype(np.float32).reshape(NT, 128, 16).transpose(1, 0, 2).reshape(128, NT * 16)
    ropes = np.concatenate([-sin, sin], axis=1).astype(np.float32).reshape(NT, 128, 16).transpose(1, 0, 2).reshape(128, NT * 16)
    return dict(c_ident=ident, c_j2=j2, c_cmask=cmask, c_maska=np.ascontiguousarray(maska),
                c_ropec=np.ascontiguousarray(ropec), c_ropes=np.ascontiguousarray(ropes))


def _weights(norm_w, w_in, q_norm_a, k_norm_a, sink_a, q_norm_b, k_norm_b, rpb_b, w_out, w_ple, w_ple_gate):
    f = lambda a: np.ascontiguousarray(np.asarray(a, dtype=np.float32))
    return dict(norm_w=f(norm_w[0]), w_in=f(w_in[0]), q_norm_a=f(q_norm_a[0]), k_norm_a=f(k_norm_a[0]),
                sink_a=f(sink_a[0]), q_norm_b=f(q_norm_b[0]), k_norm_b=f(k_norm_b[0]),
                rpb_b=f(np.asarray(rpb_b[0]).reshape(120, 31)), w_out=f(w_out[0]), w_ple=f(w_ple[0]),
                w_gate=f(w_ple_gate[0]))


def kernel(x_prompt, x_sample, p_prompt, p_sample, norm_w, w_in, q_norm_a, k_norm_a, sink_a,
           q_norm_b, k_norm_b, rpb_b, w_out, w_ple, w_ple_gate):
    n = 8
    x_prompt = np.asarray(x_prompt, dtype=np.float32)
    x_sample = np.asarray(x_sample, dtype=np.float32)
    p_prompt = np.asarray(p_prompt, dtype=np.float32)[0]
    p_sample = np.asarray(p_sample, dtype=np.float32)[0]
    shared = _weights(norm_w, w_in, q_norm_a, k_norm_a, sink_a, q_norm_b, k_norm_b, rpb_b, w_out, w_ple, w_ple_gate)
    shared.update(_consts())
    nc = build(4, 1)
    in_maps = []
    for c in range(n):
        m = dict(shared)
        m["xp"] = np.ascontiguousarray(x_prompt[4 * c:4 * c + 4])
        m["pp"] = np.ascontiguousarray(p_prompt[4 * c:4 * c + 4])
        m["xs"] = np.ascontiguousarray(x_sample[c:c + 1])
        m["ps"] = np.ascontiguousarray(p_sample[c:c + 1])
        in_maps.append(m)
    res = run_bass_kernel_spmd(nc, in_maps, core_ids=list(range(n)))
    y_prompt = np.concatenate([np.asarray(r["yp"], dtype=np.float32) for r in res.results], axis=0)
    y_sample = np.concatenate([np.asarray(r["ys"], dtype=np.float32) for r in res.results], axis=0)
    return (y_prompt, y_sample)
```

```python
import numpy as np
from contextlib import ExitStack
import concourse.bass as bass
import concourse.mybir as mybir
from concourse.bass_utils import run_bass_kernel_spmd

F32 = mybir.dt.float32
BF16 = mybir.dt.bfloat16
AF = mybir.ActivationFunctionType
ALU = mybir.AluOpType
AX = mybir.AxisListType

D = 1024
S = 2048
NT = 16
INW = 3328
EPS = 1e-6
KR = 8
QR = 5
TRANSITIVE = True

WMAP = [(0, 0, 640), (640, 1280, 1024), (1664, 640, 128), (1792, 2304, 512),
        (2304, 768, 512), (2816, 2816, 512)]


def _b_tiles(t):
    out = []
    for u in range(NT):
        pat = []
        for kp in range(2):
            for qp in range(2):
                r = 2 * t + qp
                rs = min(max(r - 4, 0), 24)
                kr = 2 * u + kp
                pat.append(rs <= kr <= rs + 7)
        if any(pat):
            out.append((u, (u - t, tuple(pat))))
    return out


def _variants():
    vs = []
    for t in range(NT):
        for _, v in _b_tiles(t):
            if v not in vs:
                vs.append(v)
    return vs


VARIANTS = _variants()
NV = len(VARIANTS)


class Sched:
    def __init__(self, nc, es):
        self.nc = nc
        self.es = es
        self.eng = {'pe': nc.tensor, 'act': nc.scalar, 'dve': nc.vector, 'pool': nc.gpsimd, 'sp': nc.sync}
        self.sem = {}
        self.cnt = {}
        self.isdma = {}
        self.known = {e: {} for e in self.eng}
        self.res = {}
        self.pending = None
        self.snap = {}
        self.nstand = 0
        self.transitive = TRANSITIVE
        for e in self.eng:
            if e != 'sp':
                self.add_sem(e, False)

    def add_sem(self, name, isdma):
        self.sem[name] = self.es.enter_context(self.nc.semaphore("s_" + name))
        self.cnt[name] = 0
        self.isdma[name] = isdma

    def _need(self, e, ev):
        if ev is None:
            return
        src, v = ev
        if e == 'pe' and src == 'pe':
            return
        if self.isdma[src]:
            v = self.cnt[src]
        if self.known[e].get(src, 0) >= v:
            return
        if self.pending is not None:
            self.pending.append((src, v))
        else:
            self.eng[e].wait_ge(self.sem[src], v)
            self.nstand += 1
        self.known[e][src] = v
        if self.transitive:
            sn = self.snap.get((src, v))
            if sn:
                ke = self.known[e]
                for k2, v2 in sn.items():
                    if ke.get(k2, 0) < v2:
                        ke[k2] = v2

    def _deps(self, e, reads, writes):
        for r in reads:
            st = self.res.get(r)
            if st:
                self._need(e, st[0])
        for w in writes:
            st = self.res.get(w)
            if st:
                self._need(e, st[0])
                for src, v in list(st[1].items()):
                    self._need(e, (src, v))

    def _record(self, ev, reads, writes):
        src, v = ev
        for r in reads:
            st = self.res.setdefault(r, [None, {}])
            if st[1].get(src, 0) < v:
                st[1][src] = v
        for w in writes:
            self.res[w] = [ev, {}]

    def _collect(self, e, reads, writes):
        self.pending = []
        self._deps(e, reads, writes)
        waits, self.pending = self.pending, None
        last = {}
        for src, v in waits:
            last[src] = max(last.get(src, 0), v)
        waits = list(last.items())
        for src, v in waits[:-1]:
            self.eng[e].wait_ge(self.sem[src], v)
            self.nstand += 1
        return waits[-1] if waits else None

    def op(self, e, reads, writes, fn):
        w = self._collect(e, reads, writes)
        inst = fn(self.eng[e])
        if w is not None:
            inst._wait_ge(self.sem[w[0]], w[1])
        self.cnt[e] += 1
        inst.then_inc(self.sem[e], 1)
        self.snap[(e, self.cnt[e])] = dict(self.known[e])
        self._record((e, self.cnt[e]), reads, writes)

    def pe_group(self, reads, writes, fns):
        w = self._collect('pe', reads, writes)
        inst = None
        for i, fn in enumerate(fns):
            inst = fn(self.eng['pe'])
            if i == 0 and w is not None:
                inst._wait_ge(self.sem[w[0]], w[1])
        self.cnt['pe'] += 1
        inst.then_inc(self.sem['pe'], 1)
        self.snap[('pe', self.cnt['pe'])] = dict(self.known['pe'])
        self._record(('pe', self.cnt['pe']), reads, writes)

    def dma(self, q, chan, reads, writes, out, in_, serialize=True, **kw):
        if chan not in self.sem:
            self.add_sem(chan, True)
        if serialize and self.cnt[chan] > 0:
            self._need(q, (chan, self.cnt[chan]))
        self._deps(q, reads, writes)
        inst = self.eng[q].dma_start(out=out, in_=in_, **kw)
        self.cnt[chan] += 16
        inst.then_inc(self.sem[chan], 16)
        self._record((chan, self.cnt[chan]), reads, writes)

    def barrier(self, engines=None):
        for e in (engines or self.eng):
            for src in self.sem:
                if self.cnt[src] > 0:
                    self._need(e, (src, self.cnt[src]))


def build(nseq_p=4, nseq_s=1):
    nc = bass.Bass("TRN2", target_bir_lowering=False)
    nseq = nseq_p + nseq_s

    def din(name, shape):
        return nc.dram_tensor(name, list(shape), F32, kind="ExternalInput")

    xp = din("xp", [max(nseq_p, 1), S, D]).ap()
    pp = din("pp", [max(nseq_p, 1), S, 256]).ap()
    yp = nc.dram_tensor("yp", [max(nseq_p, 1), S, D], F32, kind="ExternalOutput").ap()
    if nseq_s:
        xs = din("xs", [nseq_s, S, D]).ap()
        pps = din("ps", [nseq_s, S, 256]).ap()
        ys = nc.dram_tensor("ys", [nseq_s, S, D], F32, kind="ExternalOutput").ap()
    norm_w = din("norm_w", [D]).ap()
    w_in = din("w_in", [D, INW]).ap()
    qna = din("q_norm_a", [64])
    kna = din("k_norm_a", [64])
    sink = din("sink_a", [8])
    qnb = din("q_norm_b", [64])
    knb = din("k_norm_b", [64])
    rpb = din("rpb_b", [120, 31]).ap()
    w_out = din("w_out", [D, D]).ap()
    w_ple = din("w_ple", [256, D]).ap()
    w_gate = din("w_gate", [D, D]).ap()
    c_ident = din("c_ident", [128, 128]).ap()
    c_j2 = din("c_j2", [128, 128]).ap()
    c_cmask = din("c_cmask", [128, 128]).ap()
    c_maska = din("c_maska", [128, 2, 128]).ap()
    c_ropec = din("c_ropec", [128, NT * 16]).ap()
    c_ropes = din("c_ropes", [128, NT * 16]).ap()
    pd = nc.dram_tensor("pd_scratch", [120, 128], F32, kind="Internal")

    def seq_aps(s):
        if s < nseq_p:
            return xp[s], pp[s], yp[s]
        return xs[s - nseq_p], pps[s - nseq_p], ys[s - nseq_p]

    with ExitStack() as es:
        sc = Sched(nc, es)

        def sb(name, shape, dt, stack=es):
            return stack.enter_context(nc.sbuf_tensor(name, list(shape), dt))

        Wg = sb("Wg", [128, 8, INW], BF16)
        Wo = sb("Wo", [128, 8, D], BF16)
        Wgt = sb("Wgt", [128, 8, D], BF16)
        Wp = sb("Wp", [128, 2, D], BF16)
        EB = sb("EB", [128, NV, 8, 128], BF16)
        idb = sb("idb", [128, 128], BF16)
        maskA = sb("maskA", [128, 2, 128], BF16)
        ropeC = sb("ropeC", [128, NT, 16], F32)
        ropeS = sb("ropeS", [128, NT, 16], F32)
        gpp = sb("gpp", [128, 4], F32)
        gain16 = sb("gain16", [128, 10, 16], F32)
        esink = sb("esink", [128, 8], F32)
        normw = sb("normw", [128, 8], F32)
        Qz = sb("Qz", [128, QR, 16, 128], BF16)
        Kt = sb("Kt", [128, KR, 5, 128], BF16)
        Vr = sb("Vr", [128, KR, 10, 65], BF16)
        Gr = sb("Gr", [128, QR, D], BF16)
        ps_mm = es.enter_context(nc.psum_tensor("ps_mm", [128, 2, 512], F32))
        ps_s = es.enter_context(nc.psum_tensor("ps_s", [128, 2, 1024], F32))
        ps_o = es.enter_context(nc.psum_tensor("ps_o", [128, 2, 512], F32))

        mmc = [0]

        def mm_bank():
            b = mmc[0] % 2
            mmc[0] += 1
            return b

        with ExitStack() as ies:
            wst2 = sb("wst2", [128, 2, INW], F32, ies)
            z = sb("z", [120, 128], F32, ies)
            Gt4 = sb("Gt4", [128, 3, 8, 128], F32, ies)
            ebf2 = sb("ebf2", [128, 1, 8, 128], F32, ies)
            j2 = sb("j2", [128, 128], F32, ies)
            cmask = sb("cmask", [128, 128], F32, ies)
            idf = sb("idf", [128, 128], F32, ies)
            mAf = sb("mAf", [128, 2, 128], F32, ies)
            g16 = sb("g16", [128, 2, 16], F32, ies)
            nw8 = sb("nw8", [8, 128], F32, ies)

            sc.op('pool', [], ['z'], lambda e: e.memset(z[:], 0.0))
            sc.op('pool', [], ['gpp'], lambda e: e.memset(gpp[:], 1.0))

            def mdma(chan, writes, out, in_, **kw):
                sc.dma('sp', chan, [], writes, out, in_, serialize=False, **kw)

            mdma('misc', ['z'], z[:, 48:79], rpb)
            mdma('misc', ['nw8'], nw8[:], norm_w.rearrange("(k p) -> k p", p=128))
            mdma('misc', ['idf'], idf[:], c_ident)
            mdma('misc', ['j2'], j2[:], c_j2)
            mdma('misc', ['cmask'], cmask[:], c_cmask)
            sc.dma('sp', 'pdw', ['z'], ['pd'], pd.ap(), z[:])
            sc.op('pool', [], ['Qz'], lambda e: e.memset(Qz[:], 0.0))
            sc.op('pool', [], ['Vr'], lambda e: e.memset(Vr[:], 1.0))
            sc.pe_group(['nw8', 'idf'], [('psmm', 0)], [
                lambda e: e.matmul(ps_mm[:, 0, 0:8], lhsT=nw8[0:8, :], rhs=idf[0:8, 0:8], start=True, stop=True)])
            sc.op('dve', [('psmm', 0)], ['normw'], lambda e: e.tensor_copy(out=normw[:], in_=ps_mm[:, 0, 0:8]))
            sc.op('dve', ['idf'], ['idb'], lambda e: e.tensor_copy(out=idb[:], in_=idf[:]))

            gn = sb("gn", [4, 128], F32, ies)

            def emit_misc2():
                mdma('misc2', ['mAf'], mAf[:], c_maska)
                mdma('misc2', ['ropeC'], ropeC[:].rearrange("p t c -> p (t c)"), c_ropec)
                mdma('misc2', ['ropeS'], ropeS[:].rearrange("p t c -> p (t c)"), c_ropes)
                mdma('misc2', ['esink'], esink[:], bass.AP(sink, 0, [[0, 128], [1, 8]]))
                mdma('misc2', ['g16'], g16[:, 0, :], bass.AP(qna, 0, [[0, 128], [1, 16]]))
                mdma('misc2', ['g16'], g16[:, 1, :], bass.AP(kna, 0, [[0, 128], [1, 16]]))
                for row, src in enumerate([qna, kna, qnb, knb]):
                    for half in range(2):
                        mdma('misc2', ['gn'], gn[row:row + 1, half * 64:(half + 1) * 64], bass.AP(src, 0, [[0, 1], [1, 64]]))

            def gen_w():
                wi = 0
                for kc in range(8):
                    i = wi % 2
                    wi += 1
                    wst = wst2[:, i, :]
                    sc.dma('sp', 'wst%d' % i, [], [('wst', i)], wst, w_in[kc * 128:(kc + 1) * 128, :])
                    for (dst, src, w) in WMAP:
                        if dst < 1664:
                            sc.op('dve', [('wst', i), 'normw'], ['W'], lambda e, dst=dst, src=src, w=w, kc=kc, wst=wst: e.tensor_scalar(
                                out=Wg[:, kc, dst:dst + w], in0=wst[:, src:src + w], scalar1=normw[:, kc:kc + 1],
                                scalar2=None, op0=ALU.mult))
                        else:
                            sc.op('act', [('wst', i), 'normw'], ['W'], lambda e, dst=dst, src=src, w=w, kc=kc, wst=wst: e.activation(
                                out=Wg[:, kc, dst:dst + w], in_=wst[:, src:src + w], func=AF.Copy, scale=normw[:, kc:kc + 1]))
                    if kc == 1:
                        emit_misc2()
                    yield

            def eb_dma(vi):
                delta, pat = VARIANTS[vi]
                gb = vi % 3
                for qp in range(2):
                    for kp in range(2):
                        drow = min(max(2 * delta + 7 + kp - qp, 0), 14)
                        sc.dma('act', 'gt%d' % gb, ['pd'], [('Gt', gb)],
                               Gt4[qp * 64:(qp + 1) * 64, gb, :, kp * 64:(kp + 1) * 64],
                               bass.AP(pd, drow * 128, [[1, 64], [15 * 128, 8], [1, 64]]), serialize=False)

            def gen_eb():
                for vi, (delta, pat) in enumerate(VARIANTS):
                    bb = vi % 2
                    gb = vi % 3
                    Gt = Gt4[:, gb]
                    ebf = ebf2[:, 0]
                    for half in range(2):
                        sc.pe_group([('Gt', gb), 'j2'], [('pss', bb)], [
                            (lambda e, h=h: e.matmul(ps_s[:, bb, h * 128:(h + 1) * 128], lhsT=Gt[:, h, :], rhs=j2[:],
                                                     start=True, stop=True)) for h in range(half * 4, half * 4 + 4)])
                    sc.op('act', [('pss', bb)], [('ebf', 0)], lambda e: e.activation(
                        out=ebf.rearrange("p h q -> p (h q)"), in_=ps_s[:, bb, :], func=AF.Exp))
                    sc.op('dve', [('ebf', 0), 'cmask'], ['EB'], lambda e, vi=vi: e.tensor_tensor(
                        out=EB[:, vi, :, :], in0=ebf, in1=cmask[:].unsqueeze(1).to_broadcast([128, 8, 128]), op=ALU.mult))
                    for kp in range(2):
                        for qp in range(2):
                            if not pat[kp * 2 + qp]:
                                sc.op('pool', [], ['EB'], lambda e, vi=vi, kp=kp, qp=qp: e.memset(
                                    EB[kp * 64:(kp + 1) * 64, vi, :, qp * 64:(qp + 1) * 64], 0.0))
                    if vi + 3 < NV:
                        eb_dma(vi + 3)
                    yield

            for vi in range(min(3, NV)):
                eb_dma(vi)
            gens = [gen_w(), gen_eb()]
            while gens:
                for gen in list(gens):
                    try:
                        next(gen)
                    except StopIteration:
                        gens.remove(gen)
            sc.pe_group(['gn', 'idf'], [('psmm', 1)], [
                lambda e: e.matmul(ps_mm[:, 1, 0:4], lhsT=gn[0:4, :], rhs=idf[0:4, 0:4], start=True, stop=True)])
            sc.op('dve', [('psmm', 1)], ['gpp'], lambda e: e.tensor_copy(out=gpp[:], in_=ps_mm[:, 1, 0:4]))
            for half in range(2):
                sc.op('dve', ['gpp'], ['gpp'], lambda e, half=half: e.memset(gpp[half * 64:half * 64 + 16, 0:2], 1.0))
            sc.op('act', ['esink'], ['esink'], lambda e: e.activation(out=esink[:], in_=esink[:], func=AF.Exp))
            sc.op('dve', ['mAf'], ['maskA'], lambda e: e.tensor_copy(out=maskA[:], in_=mAf[:]))
            sc.op('dve', ['g16'], ['gain16'], lambda e: e.tensor_copy(
                out=gain16[:, 0:8, :], in_=g16[:, 0:1, :].to_broadcast([128, 8, 16])))
            sc.op('dve', ['g16'], ['gain16'], lambda e: e.tensor_copy(
                out=gain16[:, 8:10, :], in_=g16[:, 1:2, :].to_broadcast([128, 2, 16])))
            sc.barrier()
        xt = sb("xt", [128, 1, D], F32)
        xT = sb("xT", [128, 8, 128], BF16)
        uA = sb("uA", [128, 640], F32)
        uB = sb("uB", [128, 1024], F32)
        sq = sb("sq", [128, 1024], F32)
        qn2A = sb("qn2A", [128, 640], BF16)
        qn2B = sb("qn2B", [128, 1024], BF16)
        xb = qn2B
        otmp = sb("otmp", [128, 512], F32)
        r0 = sb("r0", [128, 10, 16], F32)
        r1 = sb("r1", [128, 10, 16], F32)
        r2 = sb("r2", [128, 10, 16], F32)
        stx = sb("stx", [128, 8], F32)
        sth = sb("sth", [128, 3, 16], F32)
        den = sb("den", [128, 2, 8], F32)
        PT = sb("PT", [128, 3, 1024], BF16)
        x1b = sb("x1b", [128, D], BF16)
        mix = sb("mix", [128, D], BF16)
        mT = sb("mT", [128, 8, 128], BF16)
        xr = sb("xr", [128, 2, D], F32)
        pt = sb("pt", [128, 2, 256], F32)
        pb = sb("pb", [128, 256], BF16)
        pT = sb("pT", [128, 2, 128], BF16)

        def psT(bank):
            return ps_mm[:, bank, :].bitcast(BF16)

        def transposes(src_blocks, bank, reads):
            pv = psT(bank)
            sc.pe_group(reads + ['idb'], [('psmm', bank)], [
                (lambda e, i=i, blk=blk: e.transpose(out=pv[:, i * 128:(i + 1) * 128], in_=blk, identity=idb[:]))
                for i, blk in enumerate(src_blocks)])
            return pv

        NG = nseq * NT

        def g_aps(g):
            xa, pa, ya = seq_aps(g // NT)
            t = g % NT
            return xa[t * 128:(t + 1) * 128, :], pa[t * 128:(t + 1) * 128, :], ya[t * 128:(t + 1) * 128, :], t

        def load_x(g):
            xa, _, _, _ = g_aps(g)
            sc.dma('sp', 'xt0', [], [('xt', 0)], xt[:, 0, :], xa)

        def load_r(g):
            xa, _, _, _ = g_aps(g)
            slot = g % 2
            sc.dma('sp', 'xr%d' % slot, [], [('xr', slot)], xr[:, slot, :], xa)

        def proj(g):
            _, _, _, j = g_aps(g)
            qs, ks = g % QR, g % KR
            xs_ = xt[:, 0, :]
            sc.op('act', [('xt', 0)], ['sq', 'sqhi', 'stx'], lambda e: e.activation(
                out=sq[:], in_=xs_, func=AF.Square, accum_out=stx[:, 0:1]))
            sc.op('dve', [('xt', 0)], ['qn2B'], lambda e: e.tensor_copy(out=xb[:], in_=xs_))
            if g + 1 < NG:
                load_x(g + 1)
            sc.op('dve', ['stx'], ['stx1'], lambda e: e.tensor_scalar(
                out=stx[:, 1:2], in0=stx[:, 0:1], scalar1=1.0 / D, scalar2=EPS, op0=ALU.mult, op1=ALU.add))
            sc.op('act', ['stx1'], ['stx2'], lambda e: e.activation(out=stx[:, 2:3], in_=stx[:, 1:2], func=AF.Ln))
            sc.op('act', ['stx2'], ['stx3'], lambda e: e.activation(out=stx[:, 3:4], in_=stx[:, 2:3], func=AF.Exp, scale=-0.5))
            sc.op('dve', ['stx1'], ['stx4'], lambda e: e.tensor_scalar(
                out=stx[:, 4:5], in0=stx[:, 1:2], scalar1=EPS, scalar2=None, op0=ALU.mult))
            sc.op('dve', ['stx3'], ['stx5'], lambda e: e.tensor_scalar(
                out=stx[:, 5:6], in0=stx[:, 3:4], scalar1=-1.0, scalar2=None, op0=ALU.mult))
            rstd = stx[:, 3:4]
            nrstd = stx[:, 5:6]
            yield
            bank = mm_bank()
            pv = transposes([xb[:, i * 128:(i + 1) * 128] for i in range(8)], bank, ['qn2B'])
            sc.op('dve', [('psmm', bank)], ['xT'], lambda e: e.tensor_copy(
                out=xT[:].rearrange("p k t -> p (k t)"), in_=pv[:, 0:1024]))
            yield

            def group(c0, w):
                bank = mm_bank()
                sc.pe_group(['xT', 'W'], [('psmm', bank)], [
                    (lambda e, kc=kc: e.matmul(ps_mm[:, bank, 0:w], lhsT=xT[:, kc, :], rhs=Wg[:, kc, c0:c0 + w],
                                               start=(kc == 0), stop=(kc == 7))) for kc in range(8)])
                return bank

            def qknorm(ubuf, uname, nh, eps_ap):
                w = nh * 64
                sc.op('act', [uname], ['sq', 'sqhi'], lambda e: e.activation(out=sq[:, 0:w], in_=ubuf[:, 0:w], func=AF.Square))
                sc.op('dve', ['sq', 'sqhi'], ['sth0'], lambda e: e.reduce_sum(
                    out=sth[:, 0, 0:nh], in_=sq[:, 0:w].rearrange("p (h d) -> p h d", d=64), axis=AX.X))
                sc.op('dve', ['sth0', 'stx4'], ['sth0'], lambda e: e.tensor_scalar(
                    out=sth[:, 0, 0:nh], in0=sth[:, 0, 0:nh], scalar1=1.0 / 64, scalar2=eps_ap, op0=ALU.mult, op1=ALU.add))
                sc.op('act', ['sth0'], ['sth1'], lambda e: e.activation(out=sth[:, 1, 0:nh], in_=sth[:, 0, 0:nh], func=AF.Ln))
                sc.op('act', ['sth1'], ['sth2'], lambda e: e.activation(
                    out=sth[:, 2, 0:nh], in_=sth[:, 1, 0:nh], func=AF.Exp, scale=-0.5))

            b0 = group(0, 512)
            sc.op('act', [('psmm', b0)], ['uA'], lambda e: e.activation(out=uA[:, 0:512], in_=ps_mm[:, b0, :], func=AF.Copy))
            yield
            b1 = group(512, 128)
            sc.op('act', [('psmm', b1)], ['uA'], lambda e: e.activation(out=uA[:, 512:640], in_=ps_mm[:, b1, 0:128], func=AF.Copy))
            qknorm(uA, 'uA', 10, stx[:, 4:5])
            u3 = uA[:, 0:640].rearrange("p (h d) -> p h d", d=64)
            sc.op('dve', ['uA', 'sth2'], ['qn2A'], lambda e: e.tensor_tensor(
                out=qn2A[:, 0:512].rearrange("p (g kv d) -> p kv g d", kv=2, d=64),
                in0=uA[:, 0:512].rearrange("p (kv g d) -> p kv g d", kv=2, d=64),
                in1=sth[:, 2, 0:8].rearrange("p (kv g) -> p kv g", kv=2).unsqueeze(3).to_broadcast([128, 2, 4, 64]),
                op=ALU.mult))
            sc.op('dve', ['uA', 'sth2'], ['qn2A'], lambda e: e.tensor_tensor(
                out=qn2A[:, 512:640].rearrange("p (h d) -> p h d", d=64), in0=u3[:, 8:10, :],
                in1=sth[:, 2, 8:10].unsqueeze(2).to_broadcast([128, 2, 64]), op=ALU.mult))
            sc.op('dve', ['uA', 'sth2'], ['r0'], lambda e: e.tensor_tensor(
                out=r0[:], in0=u3[:, :, 0:16], in1=sth[:, 2, 0:10].unsqueeze(2).to_broadcast([128, 10, 16]), op=ALU.mult))
            sc.op('pool', ['r0', 'gain16'], ['r0'], lambda e: e.tensor_tensor(out=r0[:], in0=r0[:], in1=gain16[:], op=ALU.mult))
            sc.op('pool', ['r0', 'ropeC'], ['r1'], lambda e: e.tensor_tensor(
                out=r1[:], in0=r0[:], in1=ropeC[:, j:j + 1, :].to_broadcast([128, 10, 16]), op=ALU.mult))
            sc.op('pool', ['r0', 'ropeS'], ['r2'], lambda e: e.tensor_tensor(
                out=r2[:, :, 0:8], in0=r0[:, :, 8:16], in1=ropeS[:, j:j + 1, 0:8].to_broadcast([128, 10, 8]), op=ALU.mult))
            sc.op('pool', ['r0', 'ropeS'], ['r2'], lambda e: e.tensor_tensor(
                out=r2[:, :, 8:16], in0=r0[:, :, 0:8], in1=ropeS[:, j:j + 1, 8:16].to_broadcast([128, 10, 8]), op=ALU.mult))
            sc.op('pool', ['r1', 'r2'], ['qn2A'], lambda e: e.tensor_tensor(
                out=qn2A[:, 0:512].rearrange("p (g kv d) -> p kv g d", kv=2, d=64)[:, :, :, 0:16],
                in0=r1[:, 0:8, :].rearrange("p (kv g) d -> p kv g d", kv=2),
                in1=r2[:, 0:8, :].rearrange("p (kv g) d -> p kv g d", kv=2), op=ALU.add))
            sc.op('pool', ['r1', 'r2'], ['qn2A'], lambda e: e.tensor_tensor(
                out=qn2A[:, 512:640].rearrange("p (h d) -> p h d", d=64)[:, :, 0:16],
                in0=r1[:, 8:10, :], in1=r2[:, 8:10, :], op=ALU.add))
            yield
            for gi in range(2):
                bq = group(640 + gi * 512, 512)
                sc.op('act', [('psmm', bq)], ['uB'], lambda e, bq=bq, gi=gi: e.activation(
                    out=uB[:, gi * 512:(gi + 1) * 512], in_=ps_mm[:, bq, :], func=AF.Copy))
                if gi == 0:
                    yield
            qknorm(uB, 'uB', 16, stx[:, 4:5])
            sc.op('dve', ['uB', 'sth2'], ['qn2B'], lambda e: e.tensor_tensor(
                out=qn2B[:].rearrange("p (h d) -> p h d", d=64), in0=uB[:].rearrange("p (h d) -> p h d", d=64),
                in1=sth[:, 2, 0:16].unsqueeze(2).to_broadcast([128, 16, 64]), op=ALU.mult))
            yield
            bv = group(1664, 512)
            sc.op('act', [('psmm', bv), 'stx3'], [('V', ks)], lambda e: e.activation(
                out=Vr[:, ks, 0:8, 0:64], in_=ps_mm[:, bv, :].rearrange("p (h d) -> p h d", d=64),
                func=AF.Identity, scale=rstd))
            yield
            bv2 = group(2176, 128)
            sc.op('act', [('psmm', bv2), 'stx3'], [('V', ks)], lambda e: e.activation(
                out=Vr[:, ks, 8:10, 0:64], in_=ps_mm[:, bv2, 0:128].rearrange("p (h d) -> p h d", d=64),
                func=AF.Identity, scale=rstd))
            for gi in range(2):
                gtmp = sq[:, gi * 512:(gi + 1) * 512]
                zcb = uB[:, 0:512] if gi == 0 else uA[:, 0:512]
                zname = 'uB' if gi == 0 else 'uA'
                sqn = 'sq' if gi == 0 else 'sqhi'
                bg = group(2304 + gi * 512, 512)
                sc.op('act', [('psmm', bg), 'stx5'], [sqn], lambda e, bg=bg, gtmp=gtmp: e.activation(
                    out=gtmp, in_=ps_mm[:, bg, :], func=AF.Exp, scale=nrstd))
                sc.op('dve', [('psmm', bg), 'stx3', sqn], [zname], lambda e, bg=bg, zcb=zcb: e.tensor_scalar(
                    out=zcb, in0=ps_mm[:, bg, :], scalar1=rstd, scalar2=None, op0=ALU.mult))
                sc.op('act', [sqn], [sqn], lambda e, gtmp=gtmp: e.activation(out=gtmp, in_=gtmp, func=AF.Ln, bias=1.0))
                sc.op('act', [sqn], [sqn], lambda e, gtmp=gtmp: e.activation(out=gtmp, in_=gtmp, func=AF.Exp, scale=-1.0))
                sc.op('pool', [zname, sqn], [('G', qs)], lambda e, gi=gi, gtmp=gtmp, zcb=zcb: e.tensor_tensor(
                    out=Gr[:, qs, gi * 512:(gi + 1) * 512], in0=zcb, in1=gtmp, op=ALU.mult))
                yield
            bank = mm_bank()
            pv = transposes([qn2A[:, i * 128:(i + 1) * 128] for i in range(5)], bank, ['qn2A'])
            pv3 = pv[:, 0:640].rearrange("p (b t) -> p b t", t=128)
            sc.op('dve', [('psmm', bank), 'gpp'], [('Qz', qs)], lambda e: e.tensor_scalar(
                out=Qz[0:64, qs, 0:4, :], in0=pv3[0:64, 0:4, :], scalar1=gpp[0:64, 0:1], scalar2=None, op0=ALU.mult))
            sc.op('dve', [('psmm', bank), 'gpp'], [('Qz', qs)], lambda e: e.tensor_scalar(
                out=Qz[64:128, qs, 4:8, :], in0=pv3[64:128, 0:4, :], scalar1=gpp[64:128, 0:1], scalar2=None, op0=ALU.mult))
            sc.op('dve', [('psmm', bank), 'gpp'], [('K', ks)], lambda e: e.tensor_scalar(
                out=Kt[:, ks, 0, :], in0=pv3[:, 4, :], scalar1=gpp[:, 1:2], scalar2=None, op0=ALU.mult))
            yield
            bank = mm_bank()
            pv = transposes([qn2B[:, i * 128:(i + 1) * 128] for i in range(8)], bank, ['qn2B'])
            pv3 = pv[:, 0:1024].rearrange("p (b t) -> p b t", t=128)
            qz_b = Qz[:, qs, 8:16, :].rearrange("p (i two) t -> p i two t", two=2)
            sc.op('act', [('psmm', bank), 'gpp'], [('Qz', qs)], lambda e: e.activation(
                out=qz_b[0:64, :, 0, :], in_=pv3[0:64, 0:4, :], func=AF.Copy, scale=gpp[0:64, 2:3]))
            sc.op('act', [('psmm', bank), 'gpp'], [('Qz', qs)], lambda e: e.activation(
                out=qz_b[64:128, :, 1, :], in_=pv3[64:128, 0:4, :], func=AF.Copy, scale=gpp[64:128, 2:3]))
            sc.op('act', [('psmm', bank), 'gpp'], [('K', ks)], lambda e: e.activation(
                out=Kt[:, ks, 1:5, :], in_=pv3[:, 4:8, :], func=AF.Copy, scale=gpp[:, 3:4]))
            yield

        sreg = [0]
        rot = [0]

        def o_finish(with_sink, qs, col0):
            o4 = ps_o[:, :, 0:260].rearrange("p b (h c) -> p b h c", c=65)
            d3 = den[:, 0, :].rearrange("p (b h) -> p b h", b=2)
            if with_sink:
                sc.op('dve', ['pso', 'esink'], ['den0'], lambda e: e.tensor_tensor(
                    out=d3, in0=o4[:, :, :, 64], in1=esink[:].rearrange("p (b h) -> p b h", b=2), op=ALU.add))
            else:
                sc.op('dve', ['pso'], ['den0'], lambda e: e.tensor_copy(out=d3, in_=o4[:, :, :, 64]))
            sc.op('dve', ['den0'], ['den1'], lambda e: e.reciprocal(out=den[:, 1, :], in_=den[:, 0, :]))
            r3 = den[:, 1, :].rearrange("p (b h) -> p b h", b=2)
            sc.op('dve', ['pso', 'den1'], ['otmp'], lambda e: e.tensor_tensor(
                out=otmp[:].rearrange("p (b h d) -> p b h d", b=2, d=64), in0=o4[:, :, :, 0:64],
                in1=r3.unsqueeze(3).to_broadcast([128, 2, 4, 64]), op=ALU.mult))
            sc.op('dve', ['otmp', ('G', qs)], ['mix'], lambda e: e.tensor_tensor(
                out=mix[:, col0:col0 + 512], in0=otmp[:], in1=Gr[:, qs, col0:col0 + 512], op=ALU.mult))

        def units(g):
            xa_t, pa_t, ya_t, t = g_aps(g)
            base = (g // NT) * NT
            qs = g % QR
            rslot = g % 2
            xrs = xr[:, rslot, :]
            sc.dma('sp', 'pt%d' % rslot, [], [('pt', rslot)], pt[:, rslot, :], pa_t)
            units = []
            for b in (t - 1, t, t + 1):
                if 0 <= b < NT:
                    units.append(('A', b, None if b == t else (0 if b < t else 1)))
            nA = len(units)
            for (uu, var) in _b_tiles(t):
                units.append(('B', uu, VARIANTS.index(var)))
            nU = len(units)
            regs = {}
            pts = {}

            def emit_S(i):
                kind, kt, _ = units[i]
                reg = sreg[0] % 2
                sreg[0] += 1
                regs[i] = reg
                ks = (base + kt) % KR
                if kind == 'A':
                    sc.pe_group([('K', ks), ('Qz', qs)], [('pss', reg)], [
                        (lambda e, kv=kv: e.matmul(ps_s[:, reg, kv * 512:(kv + 1) * 512], lhsT=Kt[:, ks, 0, :],
                                                   rhs=Qz[:, qs, 4 * kv:4 * kv + 4, :].rearrange("p h t -> p (h t)"),
                                                   start=True, stop=True)) for kv in range(2)])
                else:
                    for half in range(2):
                        sc.pe_group([('K', ks), ('Qz', qs)], [('pss', reg)], [
                            (lambda e, pi=pi: e.matmul(ps_s[:, reg, pi * 256:(pi + 1) * 256], lhsT=Kt[:, ks, 1 + pi, :],
                                                       rhs=Qz[:, qs, 8 + 2 * pi:8 + 2 * pi + 2, :].rearrange("p h t -> p (h t)"),
                                                       start=True, stop=True))
                            for pi in range(half * 2, half * 2 + 2)])

            def emit_E(i):
                kind, kt, aux = units[i]
                reg = regs[i]
                r = rot[0] % 3
                rot[0] += 1
                pts[i] = r
                if kind == 'A':
                    sc.op('act', [('pss', reg)], [('PT', r)], lambda e: e.activation(
                        out=PT[:, r, :], in_=ps_s[:, reg, :], func=AF.Exp, scale=0.125))
                    if aux is not None:
                        sc.op('dve', [('PT', r), 'maskA'], [('PT', r)], lambda e: e.tensor_tensor(
                            out=PT[:, r, :].rearrange("p (h q) -> p h q", q=128),
                            in0=PT[:, r, :].rearrange("p (h q) -> p h q", q=128),
                            in1=maskA[:, aux:aux + 1, :].to_broadcast([128, 8, 128]), op=ALU.mult))
                else:
                    sc.op('act', [('pss', reg)], [('PT', r)], lambda e: e.activation(
                        out=PT[:, r, :], in_=ps_s[:, reg, :], func=AF.Exp, scale=0.125))
                    sc.op('dve', [('PT', r), 'EB'], [('PT', r)], lambda e: e.tensor_tensor(
                        out=PT[:, r, :], in0=PT[:, r, :], in1=EB[:, aux, :, :].rearrange("p h q -> p (h q)"), op=ALU.mult))

            def emit_PV(i):
                kind, kt, _ = units[i]
                r = pts[i]
                ks = (base + kt) % KR
                if kind == 'A':
                    first, last = (i == 0), (i == nA - 1)
                    sc.pe_group([('PT', r), ('V', ks)], ['pso'], [
                        (lambda e, h=h: e.matmul(ps_o[:, h // 4, (h % 4) * 65:(h % 4) * 65 + 65],
                                                 lhsT=PT[:, r, h * 128:(h + 1) * 128], rhs=Vr[:, ks, h // 4, :],
                                                 start=(first and h % 4 == 0), stop=last,
                                                 skip_group_check=True)) for h in range(8)])
                else:
                    first, last = (i == nA), (i == nU - 1)
                    sc.pe_group([('PT', r), ('V', ks)], ['pso'], [
                        (lambda e, h=h: e.matmul(ps_o[:, h // 4, (h % 4) * 65:(h % 4) * 65 + 65],
                                                 lhsT=PT[:, r, h * 128:(h + 1) * 128], rhs=Vr[:, ks, 2 + h, :],
                                                 start=(first and h % 4 == 0), stop=last,
                                                 skip_group_check=True)) for h in range(8)])

            emit_S(0)
            yield
            for k in range(1, nU + 3):
                if k < nU:
                    emit_S(k)
                if k - 1 < nU:
                    emit_E(k - 1)
                if 0 <= k - 3:
                    emit_PV(k - 3)
                    if k - 3 == nA - 1:
                        o_finish(True, qs, 0)
                yield
            o_finish(False, qs, 512)
            sc.op('pool', [('pt', rslot)], ['pb'], lambda e: e.tensor_copy(out=pb[:], in_=pt[:, rslot, :]))
            yield

        def tail(g):
            xa_t, pa_t, ya_t, t = g_aps(g)
            base = (g // NT) * NT
            qs = g % QR
            rslot = g % 2
            xrs = xr[:, rslot, :]
            bank = mm_bank()
            pv = transposes([pb[:, i * 128:(i + 1) * 128] for i in range(2)], bank, ['pb'])
            sc.op('dve', [('psmm', bank)], ['pT'], lambda e: e.tensor_copy(
                out=pT[:].rearrange("p k t -> p (k t)"), in_=pv[:, 0:256]))
            yield
            bank = mm_bank()
            pv = transposes([mix[:, i * 128:(i + 1) * 128] for i in range(8)], bank, ['mix'])
            sc.op('dve', [('psmm', bank)], ['mT'], lambda e: e.tensor_copy(
                out=mT[:].rearrange("p k t -> p (k t)"), in_=pv[:, 0:1024]))
            yield
            for cg in range(2):
                bank = mm_bank()
                sc.pe_group(['mT', 'Wo'], [('psmm', bank)], [
                    (lambda e, kc=kc: e.matmul(ps_mm[:, bank, :], lhsT=mT[:, kc, :], rhs=Wo[:, kc, cg * 512:(cg + 1) * 512],
                                               start=(kc == 0), stop=(kc == 7))) for kc in range(8)])
                sc.op('dve', [('psmm', bank), ('xr', rslot)], ['x1b'], lambda e, bank=bank, cg=cg: e.tensor_tensor(
                    out=x1b[:, cg * 512:(cg + 1) * 512], in0=ps_mm[:, bank, :], in1=xrs[:, cg * 512:(cg + 1) * 512], op=ALU.add))
                sc.op('dve', [('psmm', bank), ('xr', rslot)], [('xr', rslot)], lambda e, bank=bank, cg=cg: e.tensor_tensor(
                    out=xrs[:, cg * 512:(cg + 1) * 512], in0=ps_mm[:, bank, :], in1=xrs[:, cg * 512:(cg + 1) * 512], op=ALU.add))
                yield
            bank = mm_bank()
            pv = transposes([x1b[:, i * 128:(i + 1) * 128] for i in range(8)], bank, ['x1b'])
            sc.op('dve', [('psmm', bank)], ['mT'], lambda e: e.tensor_copy(
                out=mT[:].rearrange("p k t -> p (k t)"), in_=pv[:, 0:1024]))
            yield
            for cg in range(2):
                sgs = sq[:, 512:1024]
                bg = mm_bank()
                sc.pe_group(['mT', 'Wgt'], [('psmm', bg)], [
                    (lambda e, kc=kc: e.matmul(ps_mm[:, bg, :], lhsT=mT[:, kc, :], rhs=Wgt[:, kc, cg * 512:(cg + 1) * 512],
                                               start=(kc == 0), stop=(kc == 7))) for kc in range(8)])
                sc.op('act', [('psmm', bg)], ['sqhi'], lambda e, bg=bg, sgs=sgs: e.activation(
                    out=sgs, in_=ps_mm[:, bg, :], func=AF.Exp, scale=-1.0))
                sc.op('act', ['sqhi'], ['sqhi'], lambda e, sgs=sgs: e.activation(out=sgs, in_=sgs, func=AF.Ln, bias=1.0))
                sc.op('act', ['sqhi'], ['sqhi'], lambda e, sgs=sgs: e.activation(out=sgs, in_=sgs, func=AF.Exp, scale=-1.0))
                bp = mm_bank()
                sc.pe_group(['pT', 'Wp'], [('psmm', bp)], [
                    (lambda e, kc=kc: e.matmul(ps_mm[:, bp, :], lhsT=pT[:, kc, :], rhs=Wp[:, kc, cg * 512:(cg + 1) * 512],
                                               start=(kc == 0), stop=(kc == 1))) for kc in range(2)])
                sc.op('dve', [('psmm', bp), 'sqhi'], ['sqhi'], lambda e, bp=bp, sgs=sgs: e.tensor_tensor(
                    out=sgs, in0=ps_mm[:, bp, :], in1=sgs, op=ALU.mult))
                sc.op('dve', ['sqhi', ('xr', rslot)], [('xr', rslot)], lambda e, cg=cg, sgs=sgs: e.tensor_tensor(
                    out=xrs[:, cg * 512:(cg + 1) * 512], in0=xrs[:, cg * 512:(cg + 1) * 512], in1=sgs, op=ALU.add))
                yield
            sc.dma('sp', 'y%d' % rslot, [('xr', rslot)], [], ya_t, xrs)
            yield

        LA = 4

        def run(gens):
            gens = list(gens)
            while gens:
                for gen in list(gens):
                    try:
                        next(gen)
                    except StopIteration:
                        gens.remove(gen)

        def gen_late():
            wi = 0
            for (wd, wsb, nk, rn) in [(w_out, Wo, 8, 'Wo'), (w_gate, Wgt, 8, 'Wgt'), (w_ple, Wp, 2, 'Wp')]:
                for kc in range(nk):
                    i = wi % 2
                    wi += 1
                    sc.dma('sp', 'xr%d' % i, [], [('xr', i)], xr[:, i, :], wd[kc * 128:(kc + 1) * 128, :])
                    sc.op('pool', [('xr', i)], [rn], lambda e, wsb=wsb, kc=kc, i=i: e.tensor_copy(
                        out=wsb[:, kc, :], in_=xr[:, i, :]))
                    yield

        late = gen_late()
        late_alive = [True]

        def late_step():
            if late_alive[0]:
                try:
                    next(late)
                except StopIteration:
                    late_alive[0] = False

        load_x(0)
        for g in range(min(LA, NG)):
            gp = proj(g)
            while True:
                try:
                    next(gp)
                except StopIteration:
                    break
                late_step()
        while late_alive[0]:
            late_step()
        load_r(0)
        if NG > 1:
            load_r(1)
        for g in range(NG + 1):
            gens = []
            if g < NG:
                gens.append(units(g))
            t = g % NT
            deferred = None
            if g < NG and g + LA < NG:
                gens.append(proj(g + LA))
            if g - 1 >= 0:
                gens.append(tail(g - 1))
            run(gens)
            if deferred is not None:
                run([deferred])
            if g - 1 >= 0 and g + 1 < NG:
                load_r(g + 1)
        for ch in ('y0', 'y1'):
            if ch in sc.sem:
                sc.eng['sp'].wait_ge(sc.sem[ch], sc.cnt[ch])
    return nc


def _consts():
    ident = np.eye(128, dtype=np.float32)
    j64 = np.eye(64, dtype=np.float32)[::-1]
    j2 = np.zeros((128, 128), np.float32)
    j2[0:64, 0:64] = j64
    j2[64:128, 64:128] = j64
    kc = np.arange(64)[:, None]
    qc = np.arange(64)[None, :]
    cs = np.clip(qc - 8, 0, 48)
    cv = ((kc >= cs) & (kc < cs + 16)).astype(np.float32)
    cmask = np.tile(cv, (2, 2)).astype(np.float32)
    k = np.arange(128)[:, None]
    q = np.arange(128)[None, :]
    maska = np.stack([(k >= q), (k <= q)], axis=1).astype(np.float32)
    inv_freq = np.power(np.float32(500000.0), -np.arange(0, 16, 2, dtype=np.float32) / np.float32(16)).astype(np.float32)
    ang = (np.arange(S, dtype=np.float32)[:, None] * inv_freq[None, :]).astype(np.float32)
    cos = np.cos(ang).astype(np.float32)
    sin = np.sin(ang).astype(np.float32)
    ropec = np.concatenate([cos, cos], axis=1).astype(np.float32).reshape(NT, 128, 16).transpose(1, 0, 2).reshape(128, NT * 16)
    ropes = np.concatenate([-sin, sin], axis=1).astype(np.float32).reshape(NT, 128, 16).transpose(1, 0, 2).reshape(128, NT * 16)
    return dict(c_ident=ident, c_j2=j2, c_cmask=cmask, c_maska=np.ascontiguousarray(maska),
                c_ropec=np.ascontiguousarray(ropec), c_ropes=np.ascontiguousarray(ropes))


def _weights(norm_w, w_in, q_norm_a, k_norm_a, sink_a, q_norm_b, k_norm_b, rpb_b, w_out, w_ple, w_ple_gate):
    f = lambda a: np.ascontiguousarray(np.asarray(a, dtype=np.float32))
    return dict(norm_w=f(norm_w[0]), w_in=f(w_in[0]), q_norm_a=f(q_norm_a[0]), k_norm_a=f(k_norm_a[0]),
                sink_a=f(sink_a[0]), q_norm_b=f(q_norm_b[0]), k_norm_b=f(k_norm_b[0]),
                rpb_b=f(np.asarray(rpb_b[0]).reshape(120, 31)), w_out=f(w_out[0]), w_ple=f(w_ple[0]),
                w_gate=f(w_ple_gate[0]))


def kernel(x_prompt, x_sample, p_prompt, p_sample, norm_w, w_in, q_norm_a, k_norm_a, sink_a,
           q_norm_b, k_norm_b, rpb_b, w_out, w_ple, w_ple_gate):
    n = 8
    x_prompt = np.asarray(x_prompt, dtype=np.float32)
    x_sample = np.asarray(x_sample, dtype=np.float32)
    p_prompt = np.asarray(p_prompt, dtype=np.float32)[0]
    p_sample = np.asarray(p_sample, dtype=np.float32)[0]
    shared = _weights(norm_w, w_in, q_norm_a, k_norm_a, sink_a, q_norm_b, k_norm_b, rpb_b, w_out, w_ple, w_ple_gate)
    shared.update(_consts())
    nc = build(4, 1)
    in_maps = []
    for c in range(n):
        m = dict(shared)
        m["xp"] = np.ascontiguousarray(x_prompt[4 * c:4 * c + 4])
        m["pp"] = np.ascontiguousarray(p_prompt[4 * c:4 * c + 4])
        m["xs"] = np.ascontiguousarray(x_sample[c:c + 1])
        m["ps"] = np.ascontiguousarray(p_sample[c:c + 1])
        in_maps.append(m)
    res = run_bass_kernel_spmd(nc, in_maps, core_ids=list(range(n)))
    y_prompt = np.concatenate([np.asarray(r["yp"], dtype=np.float32) for r in res.results], axis=0)
    y_sample = np.concatenate([np.asarray(r["ys"], dtype=np.float32) for r in res.results], axis=0)
    return (y_prompt, y_sample)
```
